# Optimizing a Trainium2 kernel written in Bass

```python
import jax, jax.numpy as jnp
from jax import lax
import numpy as np

D_MODEL = 1024
BATCH = 8
SEQ = 2048
DEPTH = 2

N_MIXERS = 2
NORM_EPS = 1e-6

GDN_K_HEADS = 8
GDN_V_HEADS = 16
GDN_HEAD_K = 128
GDN_HEAD_V = 128
GDN_CONV = 4
GDN_CHUNK = 64
GDN_QK = GDN_K_HEADS * GDN_HEAD_K
GDN_V = GDN_V_HEADS * GDN_HEAD_V
GDN_CONV_DIM = 2 * GDN_QK + GDN_V
GDN_PROJ = GDN_CONV_DIM + GDN_V + 2 * GDN_V_HEADS

SC_WIDTH = 2 * D_MODEL
SC_CONV = 3
SC_PROJ = 4 * SC_WIDTH

N_GDN_LAYERS = (DEPTH + 1) // 2
N_SC_LAYERS = DEPTH // 2

kernel_name = "hybrid_gdn_shortconv_sandwich"


def rms_norm(x, w):
    xf = x.astype(jnp.float32)
    y = xf * lax.rsqrt(jnp.mean(xf * xf, axis=-1, keepdims=True) + NORM_EPS)
    return y * w.astype(jnp.float32)


def l2_normalize(x):
    xf = x.astype(jnp.float32)
    return xf * lax.rsqrt(jnp.sum(xf * xf, axis=-1, keepdims=True) + NORM_EPS)


def causal_depthwise_conv(x, w):
    width = w.shape[0]
    t = x.shape[1]
    xp = jnp.pad(x, ((0, 0), (width - 1, 0), (0, 0)))
    return sum(w[j] * xp[:, j:j + t] for j in range(width))


def chunked_gated_delta_rule(q, k, v, g, beta):
    b, h, t, dk = q.shape
    dv = v.shape[-1]
    c = GDN_CHUNK
    n = t // c
    q = q * (dk ** -0.5)
    kb = k * beta[..., None]
    vb = v * beta[..., None]
    q, k, kb = [a.reshape(b, h, n, c, dk) for a in (q, k, kb)]
    vb = vb.reshape(b, h, n, c, dv)
    g = jnp.cumsum(g.reshape(b, h, n, c), axis=-1)
    incl = jnp.tril(jnp.ones((c, c), dtype=bool))
    strict = jnp.tril(jnp.ones((c, c), dtype=bool), k=-1)
    diff = g[..., :, None] - g[..., None, :]
    decay = jnp.where(incl, jnp.exp(jnp.where(incl, diff, 0.0)), 0.0)
    a_mat = jnp.where(strict, jnp.einsum('bhnid,bhnjd->bhnij', kb, k) * decay, 0.0)
    ia = a_mat + jnp.eye(c, dtype=a_mat.dtype)
    u = lax.linalg.triangular_solve(ia, vb, left_side=True, lower=True, unit_diagonal=True)
    w = lax.linalg.triangular_solve(ia, kb * jnp.exp(g)[..., None], left_side=True, lower=True,
                                    unit_diagonal=True)
    qk = jnp.where(incl, jnp.einsum('bhnid,bhnjd->bhnij', q, k) * decay, 0.0)
    q_dec = q * jnp.exp(g)[..., None]
    g_last = g[..., -1]
    k_dec = k * jnp.exp(g_last[..., None] - g)[..., None]
    state_decay = jnp.exp(g_last)

    def step(s, xs):
        q_c, k_c, u_c, w_c, qk_c, sd_c = xs
        v_new = u_c - jnp.einsum('bhck,bhkv->bhcv', w_c, s)
        o_c = jnp.einsum('bhck,bhkv->bhcv', q_c, s) + jnp.einsum('bhij,bhjv->bhiv', qk_c, v_new)
        s = s * sd_c[..., None, None] + jnp.einsum('bhck,bhcv->bhkv', k_c, v_new)
        return s, o_c

    xs = tuple(jnp.moveaxis(a, 2, 0) for a in (q_dec, k_dec, u, w, qk, state_decay))
    s0 = jnp.zeros((b, h, dk, dv), dtype=jnp.float32)
    _, o = lax.scan(step, s0, xs)
    return jnp.moveaxis(o, 0, 2).reshape(b, h, t, dv)


def gated_deltanet_mixer(h, w_in, conv_w, a_log, dt_bias, norm_w, w_out):
    bsz, t, _ = h.shape
    proj = h @ w_in
    qkv, z, b_logit, a_logit = jnp.split(
        proj, [GDN_CONV_DIM, GDN_CONV_DIM + GDN_V, GDN_CONV_DIM + GDN_V + GDN_V_HEADS], axis=-1)
    qkv = jax.nn.silu(causal_depthwise_conv(qkv, conv_w))
    q, k, v = jnp.split(qkv, [GDN_QK, 2 * GDN_QK], axis=-1)
    rep = GDN_V_HEADS // GDN_K_HEADS
    q = jnp.repeat(l2_normalize(q.reshape(bsz, t, GDN_K_HEADS, GDN_HEAD_K)), rep, axis=2)
    k = jnp.repeat(l2_normalize(k.reshape(bsz, t, GDN_K_HEADS, GDN_HEAD_K)), rep, axis=2)
    v = v.reshape(bsz, t, GDN_V_HEADS, GDN_HEAD_V).astype(jnp.float32)
    beta = jax.nn.sigmoid(b_logit.astype(jnp.float32))
    g = -jnp.exp(a_log.astype(jnp.float32)) * jax.nn.softplus(
        a_logit.astype(jnp.float32) + dt_bias.astype(jnp.float32))
    to_bht = lambda a: jnp.swapaxes(a, 1, 2)
    o = chunked_gated_delta_rule(to_bht(q), to_bht(k), to_bht(v), to_bht(g), to_bht(beta))
    o = jnp.swapaxes(o, 1, 2)
    gate = jax.nn.silu(z.reshape(bsz, t, GDN_V_HEADS, GDN_HEAD_V).astype(jnp.float32))
    o = rms_norm(o, norm_w) * gate
    return o.reshape(bsz, t, GDN_V).astype(h.dtype) @ w_out


def short_conv_mixer(h, w_in, conv_w, w_out):
    u, b_gate, c_gate, z = jnp.split(h @ w_in, 4, axis=-1)
    y = b_gate * causal_depthwise_conv(c_gate * u, conv_w)
    return (y * jax.nn.silu(z)) @ w_out


def setup_inputs(seed: int = 0) -> dict:
    key = jax.random.key(seed)
    ks = jax.random.split(key, 12)
    f32 = jnp.float32
    x = jax.random.normal(ks[0], (BATCH, SEQ, D_MODEL), f32)
    pre_norm_w = 1.0 + 0.05 * jax.random.normal(ks[1], (DEPTH, D_MODEL), f32)
    post_norm_w = 1.0 + 0.05 * jax.random.normal(ks[2], (DEPTH, D_MODEL), f32)
    gdn_w_in = jax.random.normal(ks[3], (N_GDN_LAYERS, D_MODEL, GDN_PROJ), f32) * D_MODEL ** -0.5
    gdn_conv_w = jax.random.normal(ks[4], (N_GDN_LAYERS, GDN_CONV, GDN_CONV_DIM), f32) * GDN_CONV ** -0.5
    gdn_A_log = jnp.log(jax.random.uniform(ks[5], (N_GDN_LAYERS, GDN_V_HEADS), f32, 1.0, 16.0))
    dt = jnp.exp(jax.random.uniform(ks[6], (N_GDN_LAYERS, GDN_V_HEADS), f32,
                                    float(np.log(1e-3)), float(np.log(1e-1))))
    gdn_dt_bias = dt + jnp.log(-jnp.expm1(-dt))
    gdn_norm_w = 1.0 + 0.05 * jax.random.normal(ks[7], (N_GDN_LAYERS, GDN_HEAD_V), f32)
    gdn_w_out = jax.random.normal(ks[8], (N_GDN_LAYERS, GDN_V, D_MODEL), f32) * GDN_V ** -0.5
    sc_w_in = jax.random.normal(ks[9], (N_SC_LAYERS, D_MODEL, SC_PROJ), f32) * D_MODEL ** -0.5
    sc_conv_w = jax.random.normal(ks[10], (N_SC_LAYERS, SC_CONV, SC_WIDTH), f32) * SC_CONV ** -0.5
    sc_w_out = jax.random.normal(ks[11], (N_SC_LAYERS, SC_WIDTH, D_MODEL), f32) * SC_WIDTH ** -0.5
    return {"x": x, "pre_norm_w": pre_norm_w, "post_norm_w": post_norm_w,
            "gdn_w_in": gdn_w_in, "gdn_conv_w": gdn_conv_w, "gdn_A_log": gdn_A_log,
            "gdn_dt_bias": gdn_dt_bias, "gdn_norm_w": gdn_norm_w, "gdn_w_out": gdn_w_out,
            "sc_w_in": sc_w_in, "sc_conv_w": sc_conv_w, "sc_w_out": sc_w_out}


def reference(x, pre_norm_w, post_norm_w, gdn_w_in, gdn_conv_w, gdn_A_log, gdn_dt_bias, gdn_norm_w,
              gdn_w_out, sc_w_in, sc_conv_w, sc_w_out):
    for i in range(DEPTH):
        h = rms_norm(x, pre_norm_w[i]).astype(x.dtype)
        j = i // N_MIXERS
        if i % N_MIXERS == 0:
            y = gated_deltanet_mixer(h, gdn_w_in[j], gdn_conv_w[j], gdn_A_log[j], gdn_dt_bias[j],
                                     gdn_norm_w[j], gdn_w_out[j])
        else:
            y = short_conv_mixer(h, sc_w_in[j], sc_conv_w[j], sc_w_out[j])
        x = x + rms_norm(y, post_norm_w[i]).astype(x.dtype)
    return x
```

```python
from contextlib import ExitStack

import numpy as np
import concourse.bass as bass
import concourse.mybir as mybir
from concourse.bass_utils import run_bass_kernel_spmd

F32 = mybir.dt.float32
BF16 = mybir.dt.bfloat16
ALU = mybir.AluOpType
AF = mybir.ActivationFunctionType

T = 2048
D = 1024
NT = T // 128
EPS = 1e-6
GDN_PROJ = 6176

ENGS = ("pe", "act", "dve", "pool", "sp")
DBG_NPAIRS = 8
DBG_NCHUNK = 16
DBG_STAGE = 9
DBG_PREP_STEPS = 0
DBG_PREP2_STEPS = 0
DBG_ROT1 = 0
DBG_SKIP = 0
N_DMA_SEMS = 40


class Op:
    __slots__ = ("eng", "fn", "deps", "signal", "cnt", "is_dma", "dslot", "dval", "is_out")


class Prog:
    def __init__(self, nc):
        self.nc = nc
        self.ops = {e: [] for e in ENGS}
        self.res = {}
        self.dma_rr = {"sp": 0, "pool": 0, "act": 0}
        self.dma_last = [None] * N_DMA_SEMS
        self.dma_val = [0] * N_DMA_SEMS
        self.out_dmas = []
        self.bar_deps = []
        self.bar_pending = set()

    def barrier(self):
        deps = [self.ops[e][-1] for e in ENGS if self.ops[e] and not self.ops[e][-1].is_dma]
        for e in ENGS:
            for o in reversed(self.ops[e]):
                if not o.is_dma:
                    if o not in deps:
                        deps.append(o)
                    break
        deps += [d for d in self.dma_last if d is not None]
        self.bar_deps = deps
        self.bar_pending = set(ENGS)

    def add(self, eng, fn, reads=(), writes=(), dma=False, is_out=False):
        op = Op()
        op.eng, op.fn, op.is_dma, op.signal, op.cnt, op.is_out = eng, fn, dma, False, 0, is_out
        deps = []
        if eng in self.bar_pending:
            self.bar_pending.discard(eng)
            deps.extend(d for d in self.bar_deps if d.is_dma or d.eng != eng or dma)

        def need(d, raw):
            if d is op:
                return
            if d.is_dma:
                deps.append(d)
            elif d.eng == eng and not dma:
                if raw and eng != "pe":
                    deps.append(d)
            else:
                deps.append(d)

        excl = [r for r in reads if isinstance(r, tuple) and r[0] == "ps"]
        if excl:
            reads = [r for r in reads if r not in excl]
            for r in excl:
                st = self.res.get(r)
                if st:
                    for d in st[0]:
                        need(d, True)
            writes = list(writes) + excl
        for r in reads:
            st = self.res.get(r)
            if st:
                for d in st[0]:
                    need(d, True)
        for r in writes:
            st = self.res.get(r)
            if st:
                for d in st[0]:
                    need(d, False)
                for d in st[1].values():
                    need(d, False)
                for d in st[2]:
                    need(d, False)
        for r in reads:
            st = self.res.setdefault(r, [[], {}, []])
            if dma:
                st[2].append(op)
            else:
                st[1][eng] = op
        for r in writes:
            self.res[r] = [[op], {}, []]
        if dma:
            half = N_DMA_SEMS // 2
            base = 0 if eng == "sp" else half
            slot = base + self.dma_rr[eng]
            self.dma_rr[eng] = (self.dma_rr[eng] + 1) % half
            prev = self.dma_last[slot]
            if prev is not None:
                deps.append(prev)
            self.dma_val[slot] += 16
            op.dslot, op.dval = slot, self.dma_val[slot]
            self.dma_last[slot] = op
            if is_out:
                self.out_dmas.append(op)
        op.deps = deps
        for d in deps:
            if not d.is_dma:
                d.signal = True
        self.ops[eng].append(op)
        return op

    def emit(self, stack):
        nc = self.nc
        esem = {e: stack.enter_context(nc.semaphore(f"s_{e}")) for e in ENGS}
        dsem = [stack.enter_context(nc.semaphore(f"s_d{i}")) for i in range(N_DMA_SEMS)]
        for e in ENGS:
            c = 0
            for op in self.ops[e]:
                if op.signal and not op.is_dma:
                    c += 1
                    op.cnt = c
        block = stack.enter_context(nc.Block())
        outs = self.out_dmas

        def run(e, engine):
            waited = {}

            def w(sem, key, val):
                if waited.get(key, 0) >= val:
                    return
                waited[key] = val
                engine.wait_ge(sem, val)

            for op in self.ops[e]:
                for d in op.deps:
                    if d.is_dma:
                        w(dsem[d.dslot], ("d", d.dslot), d.dval)
                    else:
                        w(esem[d.eng], d.eng, d.cnt)
                ins = op.fn(engine)
                if op.is_dma:
                    ins.then_inc(dsem[op.dslot], 16)
                elif op.signal:
                    ins.then_inc(esem[e], 1)
            if e == "sp":
                for d in outs:
                    w(dsem[d.dslot], ("d", d.dslot), d.dval)

        @block.tensor
        def _(engine):
            run("pe", engine)

        @block.scalar
        def _(engine):
            run("act", engine)

        @block.vector
        def _(engine):
            run("dve", engine)

        @block.gpsimd
        def _(engine):
            run("pool", engine)

        @block.sync
        def _(engine):
            run("sp", engine)


class Rot:
    def __init__(self, name, aps):
        self.name, self.aps, self.i = name, aps, 0

    def next(self):
        k = self.i % (1 if DBG_ROT1 else len(self.aps))
        self.i += 1
        return self.aps[k], (self.name, k)


def build_program(layers=(0, 1)):
    nc = bass.Bass("TRN2", target_bir_lowering=False)
    dt = lambda n, s, kind="ExternalInput": nc.dram_tensor(n, s, F32, kind=kind).ap()
    x_d = dt("x", [T, D])
    out_d = dt("out", [T, D], "ExternalOutput")
    pre_d = dt("pre_norm_w", [2, D])
    post_d = dt("post_norm_w", [2, D])
    gwin_d = dt("gdn_w_in", [D, GDN_PROJ])
    gcw_d = dt("gdn_conv_w", [4, 4096])
    galog_d = dt("gdn_A_log", [1, 16])
    gdtb_d = dt("gdn_dt_bias", [1, 16])
    gnw_d = dt("gdn_norm_w", [128, 1])
    gwout_d = dt("gdn_w_out", [2048, D])
    swin_d = dt("sc_w_in", [D, 8192])
    scw_d = dt("sc_conv_w", [3, 2048])
    swout_d = dt("sc_w_out", [2048, D])

    with ExitStack() as st:
        def mk_sb(stack):
            def sb(name, shape, dty=F32):
                return stack.enter_context(nc.sbuf_tensor(name, shape, dty))
            return sb
        sb = mk_sb(st)

        P = Prog(nc)
        hT = sb("hT", [128, 8, T], BF16)
        wout = hT[:].rearrange("p a b -> p (a b)").rearrange("p (j c) -> p j c", c=D)
        yT = sb("yT", [128, 16, T], BF16)
        ident = sb("ident", [128, 128], BF16)
        wn = sb("wn", [128, D])
        junk = sb("junk", [128, D], BF16)
        ss = sb("ss", [128, NT])
        sd = sb("sd", [128, NT])
        rstd = sb("rstd", [128, NT])
        hn = [sb(f"hn{i}", [128, D], BF16) for i in range(2)]
        psum = st.enter_context(nc.psum_tensor("psum", [128, 8, 512], F32))
        pbank = lambda b: psum[:, b, :]
        hT_all = [("hT", t) for t in range(NT)]
        BANKRES = {b: [("ps", b)] for b in range(8)}
        PS = lambda b: BANKRES[b]

        def MM(out, lhsT, rhs, reads, writes, start=True, stop=True):
            return P.add("pe", lambda e: e.matmul(out, lhsT=lhsT, rhs=rhs, start=start, stop=stop), reads, writes)

        def TR(out, in_, reads, writes):
            return P.add("pe", lambda e: e.transpose(out=out, in_=in_, identity=ident[:]), list(reads) + ["ident"], writes)

        def ACTF(out, in_, func, reads, writes, **kw):
            return P.add("act", lambda e: e.activation(out=out, in_=in_, func=func, **kw), reads, writes)

        def AMUL(out, in_, mul, reads, writes):
            return P.add("act", lambda e: e.mul(out, in_, mul), reads, writes)

        def TT(eng, out, in0, in1, op, reads, writes):
            return P.add(eng, lambda e: e.tensor_tensor(out=out, in0=in0, in1=in1, op=op), reads, writes)

        def STT(out, in0, scalar, in1, op0, op1, reads, writes):
            return P.add("dve", lambda e: e.scalar_tensor_tensor(out=out, in0=in0, scalar=scalar, in1=in1,
                                                                 op0=op0, op1=op1), reads, writes)

        def TS(eng, out, in0, s1, s2, op0, op1, reads, writes):
            return P.add(eng, lambda e: e.tensor_scalar(out=out, in0=in0, scalar1=s1, scalar2=s2, op0=op0, op1=op1),
                         reads, writes)

        P.add("pool", lambda e: e.memset(ident[:], 1.0), writes=["ident"])
        P.add("pool", lambda e: e.affine_select(out=ident[:], in_=ident[:], pattern=[[-1, 128]],
                                                compare_op=ALU.is_equal, fill=0.0, base=0,
                                                channel_multiplier=1), reads=["ident"], writes=["ident"])

        def phase_norm(l, xsrc, stats_first=False):
            P.add("sp", lambda e: e.dma_start(out=wn[:], in_=pre_d[l].partition_broadcast(128)),
                  writes=["wn"], dma=True)

            def stats(t, xa, xr):
                P.add("act", lambda e, t=t, xa=xa: e.activation(out=junk[:], in_=xa, func=AF.Square,
                                                                accum_out=ss[:, t:t + 1]),
                      reads=xr, writes=["junk", ("ss", t)])
                P.add("act", lambda e, t=t: e.activation(out=sd[:, t:t + 1], in_=ss[:, t:t + 1], func=AF.Sqrt,
                                                         scale=1.0 / D, bias=EPS),
                      reads=[("ss", t)], writes=[("sd", t)])
                P.add("dve", lambda e, t=t: e.reciprocal(out=rstd[:, t:t + 1], in_=sd[:, t:t + 1]),
                      reads=[("sd", t)], writes=[("rstd", t)])

            srcs = {}
            if stats_first:
                for t in range(NT):
                    srcs[t] = xsrc(t)
                    stats(t, *srcs[t])
            for t in range(NT):
                if t not in srcs:
                    srcs[t] = xsrc(t)
                    stats(t, *srcs[t])
                xa, xr = srcs[t]
                h = hn[t % 2]
                hr = ("hn", t % 2)
                P.add("dve", lambda e, t=t, h=h, xa=xa: e.scalar_tensor_tensor(
                    out=h[:], in0=xa, scalar=rstd[:, t:t + 1], in1=wn[:],
                    op0=ALU.mult, op1=ALU.mult), reads=xr + [("rstd", t), "wn"], writes=[hr])
                b = t % 2
                pT = pbank(b).bitcast(BF16)
                for dc in range(8):
                    P.add("pe", lambda e, h=h, dc=dc, pT=pT: e.transpose(
                        out=pT[:, dc * 128:(dc + 1) * 128], in_=h[:, dc * 128:(dc + 1) * 128], identity=ident[:]),
                        reads=[hr, "ident"], writes=PS(b))
                P.add("act", lambda e, t=t, pT=pT: e.copy(
                    out=hT[:, :, t * 128:(t + 1) * 128], in_=pT.rearrange("p (a b) -> p a b", b=128)),
                    reads=PS(b), writes=[("hT", t)])

        def phase_sc(sb, scrA, scrB):
            cw = sb("sc_cw", [128, 3, 16])
            for k in range(3):
                P.add("sp", lambda e, k=k: e.dma_start(out=cw[:, k, :], in_=scw_d[k].rearrange("(j p) -> p j", p=128),
                                                       allow_slow_non_contiguous=True),
                      writes=[("sc_cw", k)], dma=True)
            wrot = Rot("sc_w", [sb(f"sc_w{i}", [128, 8, 128], BF16) for i in range(5)])
            cu = sb("sc_cu", [128, 2 + T])
            u_sb = Rot("scrA", [scrA[:, 0:512], scrA[:, 512:1024]])
            sz = Rot("scrB", [scrB[:, 0:512], scrB[:, 512:1024]])
            t1 = Rot("sc_t1", [sb(f"sc_t1{i}", [128, 512]) for i in range(2)])
            gg = Rot("sc_g", [sb(f"sc_g{i}", [128, 512]) for i in range(2)])
            P.add("pool", lambda e: e.memset(cu[:, 0:2], 0.0), writes=["cu_pad"])
            src = swin_d.rearrange("(kc p) (ty j c) -> p kc ty j c", p=128, ty=4, c=128)
            for j in range(16):
                wts = []
                for ty in range(4):
                    wa, wr = wrot.next()
                    P.add("pool", lambda e, wa=wa, j=j, ty=ty: e.dma_start(out=wa[:], in_=src[:, :, ty, j, :]),
                          writes=[wr], dma=True)
                    wts.append((wa, wr))
                for nb in range(4):
                    c0 = nb * 512
                    hres = [("hT", 4 * nb + i) for i in range(4)]
                    bk = [(nb % 2) * 4 + ty for ty in range(4)]
                    for ty in range(4):
                        wa, wr = wts[ty]
                        for kc in range(8):
                            P.add("pe", lambda e, wa=wa, kc=kc, c0=c0, b=bk[ty]: e.matmul(
                                pbank(b), lhsT=wa[:, kc, :], rhs=hT[:, kc, c0:c0 + 512],
                                start=(kc == 0), stop=(kc == 7)),
                                reads=[wr] + hres, writes=PS(bk[ty]))
                    pu, pB, pC, pz = [pbank(b) for b in bk]
                    ru, rB, rC, rz = [PS(b) for b in bk]
                    ua, ur = u_sb.next()
                    P.add("act", lambda e, ua=ua, pu=pu: e.copy(out=ua, in_=pu), reads=ru, writes=[ur])
                    P.add("dve", lambda e, ua=ua, pC=pC, c0=c0: e.tensor_tensor(
                        out=cu[:, 2 + c0:2 + c0 + 512], in0=pC, in1=ua, op=ALU.mult),
                        reads=rC + [ur], writes=[("cu", nb)])
                    za, zr = sz.next()
                    P.add("act", lambda e, za=za, pz=pz: e.activation(out=za, in_=pz, func=AF.Silu),
                          reads=rz, writes=[zr])
                    ta, tr = t1.next()
                    halo = [("cu", nb - 1)] if nb > 0 else ["cu_pad"]
                    P.add("pool", lambda e, ta=ta, c0=c0, j=j: e.tensor_scalar(
                        out=ta[:], in0=cu[:, c0:c0 + 512], scalar1=cw[:, 0, j:j + 1], scalar2=0.0,
                        op0=ALU.mult, op1=ALU.add), reads=[("cu", nb), ("sc_cw", 0)] + halo, writes=[tr])
                    P.add("dve", lambda e, ta=ta, c0=c0, j=j: e.scalar_tensor_tensor(
                        out=ta[:], in0=cu[:, 1 + c0:1 + c0 + 512], scalar=cw[:, 1, j:j + 1], in1=ta[:],
                        op0=ALU.mult, op1=ALU.add), reads=[("cu", nb), ("sc_cw", 1), tr] + halo, writes=[tr])
                    P.add("dve", lambda e, ta=ta, c0=c0, j=j: e.scalar_tensor_tensor(
                        out=ta[:], in0=cu[:, 2 + c0:2 + c0 + 512], scalar=cw[:, 2, j:j + 1], in1=ta[:],
                        op0=ALU.mult, op1=ALU.add), reads=[("cu", nb), ("sc_cw", 2), tr], writes=[tr])
                    ga, gr = gg.next()
                    P.add("dve", lambda e, ga=ga, pB=pB, za=za: e.tensor_tensor(
                        out=ga[:], in0=pB, in1=za, op=ALU.mult), reads=rB + [zr], writes=[gr])
                    P.add("pool", lambda e, ga=ga, ta=ta, j=j, c0=c0: e.tensor_tensor(
                        out=yT[:, j, c0:c0 + 512], in0=ta[:], in1=ga[:], op=ALU.mult),
                        reads=[tr, gr], writes=[("yT", j, nb)])

        def phase_out(l, wout_d, final, xs, sb, scrA, scrB):
            P.add("sp", lambda e: e.dma_start(out=wn[:], in_=post_d[l].partition_broadcast(128)),
                  writes=["wn"], dma=True)
            wsrc = wout_d.rearrange("(j p) c -> p j c", p=128)
            for h in range(16):
                P.add("pool", lambda e, h=h: e.dma_start(out=wout[:, h:h + 1, :], in_=wsrc[:, h:h + 1, :]),
                      writes=(hT_all if h == 0 else []) + [("wout", h)], dma=True)
            ssq = sb(f"ssq{l}", [128, NT])
            sdq = sb(f"sdq{l}", [128, NT])
            rsq = sb(f"rsq{l}", [128, NT])
            tmps = [(scrA, [("scrA", 0), ("scrA", 1)]), (scrB, [("scrB", 0), ("scrB", 1)])]
            for t in range(NT):
                b0 = (t % 2) * 2
                for half in range(2):
                    for j in range(16):
                        P.add("pe", lambda e, t=t, half=half, j=j, b0=b0: e.matmul(
                            pbank(b0 + half), lhsT=yT[:, j, t * 128:(t + 1) * 128],
                            rhs=wout[:, j, half * 512:(half + 1) * 512], start=(j == 0), stop=(j == 15)),
                            reads=[("wout", j), ("wout", 0), ("yT", j, t // 4)] + (hT_all if j == 15 else []),
                            writes=PS(b0 + half))
                pp = psum[:, b0:b0 + 2, :].rearrange("p a b -> p (a b)")
                pr = PS(b0) + PS(b0 + 1)
                P.add("act", lambda e, t=t, pp=pp: e.activation(out=junk[:], in_=pp, func=AF.Square,
                                                                accum_out=ssq[:, t:t + 1]),
                      reads=pr, writes=["junk", ("ssq", t)])
                P.add("act", lambda e, t=t: e.activation(out=sdq[:, t:t + 1], in_=ssq[:, t:t + 1], func=AF.Sqrt,
                                                         scale=1.0 / D, bias=EPS),
                      reads=[("ssq", t)], writes=[("sdq", t)])
                P.add("dve", lambda e, t=t: e.reciprocal(out=rsq[:, t:t + 1], in_=sdq[:, t:t + 1]),
                      reads=[("sdq", t)], writes=[("rsq", t)])
                ta, tr = tmps[t % 2]
                P.add("dve", lambda e, t=t, ta=ta, pp=pp: e.scalar_tensor_tensor(
                    out=ta[:], in0=pp, scalar=rsq[:, t:t + 1], in1=wn[:], op0=ALU.mult, op1=ALU.mult),
                    reads=pr + [("rsq", t), "wn"], writes=tr)
                P.add("pool", lambda e, t=t, ta=ta: e.tensor_tensor(
                    out=xs[:, t, :], in0=xs[:, t, :], in1=ta[:], op=ALU.add),
                    reads=[("xs", t)] + tr, writes=[("xs", t)])
                if final:
                    P.add("sp", lambda e, t=t: e.dma_start(out=out_d[t * 128:(t + 1) * 128, :], in_=xs[:, t, :]),
                          reads=[("xs", t)], dma=True, is_out=True)

        def phase_gdn(sb, xbufs):
            BIG = 1.0e4
            gsrc = gwin_d.rearrange("(kc p) n -> p kc n", p=128)
            onesF = sb("onesF", [128, 128])
            Tri = sb("Tri", [128, 128])
            maskL = sb("maskL", [128, 128])
            maskU = sb("maskU", [128, 128])
            GR = sb("g_GR", [128, 4, 128])
            Eb = GR[:].rearrange("p a b -> p (a b)").bitcast(BF16).rearrange("p (a b) -> p a b", b=128)
            maskB = sb("maskB", [128, 7, 4, 128], BF16)
            UT0 = sb("UT0", [128, 4, 2, 128], BF16)
            UTw = [sb(f"UTw{i}", [128, 4, 2, 128], BF16) for i in range(2)]
            zerosF = sb("zerosF", [128, 128])
            sel = lambda out, in_, pat, op, fill, base, cm, reads, writes: P.add(
                "pool", lambda e: e.affine_select(out=out, in_=in_, pattern=pat, compare_op=op, fill=fill,
                                                  base=base, channel_multiplier=cm), reads, writes)
            P.add("pool", lambda e: e.memset(onesF[:], 1.0), writes=["onesF"])
            P.add("pool", lambda e: e.memset(zerosF[:], 0.0), writes=["zerosF"])
            sel(Tri[:], onesF[:], [[1, 128]], ALU.is_ge, 0.0, 0, -1, ["onesF"], ["Tri"])
            sel(maskL[:], zerosF[:], [[-1, 128]], ALU.is_ge, BIG, 0, 1, ["zerosF"], ["maskL"])
            sel(maskU[:], zerosF[:], [[1, 128]], ALU.is_ge, -BIG, 0, -1, ["zerosF"], ["maskU"])
            P.add("pool", lambda e: e.tensor_copy(out=Eb[:, 0, :], in_=ident[:]), reads=["ident"], writes=[("Eb", 0)])
            P.add("pool", lambda e: e.memset(Eb[:, 7, :], 1.0), writes=[("Eb", 7)])
            for k in range(1, 7):
                b = 1 << k
                sel(Eb[:, k, :], onesF[:], [[-b, 128 // b], [0, b]], ALU.is_ge, 0.0, 0, 1, ["onesF"], [("Eb", k)])
                sel(Eb[:, k, :], Eb[:, k, :], [[b, 128 // b], [0, b]], ALU.is_ge, 0.0, b - 1, -1,
                    [("Eb", k)], [("Eb", k)])
            for lv in range(7):
                for i in range(4):
                    TT("pool", maskB[:, lv, i, :], Eb[:, lv + 1, :], Eb[:, lv, :], ALU.subtract,
                       [("Eb", lv + 1), ("Eb", lv)], [("maskB", lv)] if i == 3 else [("maskB_part", lv, i)])
            for a in range(4):
                for b_ in range(2):
                    P.add("pool", lambda e, a=a, b_=b_: e.tensor_copy(out=UT0[:, a, b_, :], in_=ident[:]),
                          reads=["ident"], writes=["UT0"] if (a, b_) == (3, 1) else [("UT0_part", a, b_)])

            gnw = sb("gnw", [128, 1])
            P.add("sp", lambda e: e.dma_start(out=gnw[:], in_=gnw_d), writes=["gnw"], dma=True)
            gcw = sb("gcw", [128, 4, 32])
            for k in range(4):
                P.add("sp", lambda e, k=k: e.dma_start(out=gcw[:, k, :], in_=gcw_d[k].rearrange("(j p) -> p j", p=128),
                                                       allow_slow_non_contiguous=True),
                      writes=[("gcw", k)], dma=True)
            gcwr = [("gcw", k) for k in range(4)]

            sm = lambda n: sb(n, [128, 16, 16])
            beta, tmp1, dtb, eA, graw, cg, gl, eg, bgam, kd, sdec = [sm(n) for n in (
                "g_beta", "g_tmp1", "g_dtb", "g_eA", "g_graw", "g_cg", "g_gl", "g_eg", "g_bgam", "g_kd", "g_sdec")]
            fl = lambda a: a[:].rearrange("p a b -> p (a b)")
            P.add("sp", lambda e: e.dma_start(out=dtb[:], in_=bass.AP(gdtb_d.tensor, 0, [[0, 128], [0, 16], [1, 16]])),
                  writes=["g_dtb"], dma=True)
            P.add("sp", lambda e: e.dma_start(out=eA[:], in_=bass.AP(galog_d.tensor, 0, [[0, 128], [0, 16], [1, 16]])),
                  writes=["g_eA"], dma=True)
            wbg = sb("wbg", [128, 8, 32], BF16)
            P.add("pool", lambda e: e.dma_start(out=wbg[:], in_=gsrc[:, :, 6144:6176]), writes=["wbg"], dma=True)
            for c in range(16):
                for kc in range(8):
                    MM(psum[:, 0, c * 32:(c + 1) * 32], hT[:, kc, c * 128:(c + 1) * 128], wbg[:, kc, :],
                       [("hT", c), "wbg"], PS(0), start=(kc == 0), stop=(kc == 7))
            pbg = psum[:, 0, :].rearrange("p (c n) -> p c n", n=32)
            ACTF(beta[:], pbg[:, :, 0:16], AF.Sigmoid, PS(0), ["g_beta"])
            TT("dve", tmp1[:], pbg[:, :, 16:32], dtb[:], ALU.add, PS(0) + ["g_dtb"], ["g_tmp1"])
            ACTF(tmp1[:], tmp1[:], AF.Exp, ["g_tmp1"], ["g_tmp1"])
            ACTF(tmp1[:], tmp1[:], AF.Ln, ["g_tmp1"], ["g_tmp1"], bias=1.0)
            ACTF(eA[:], eA[:], AF.Exp, ["g_eA"], ["g_eA"])
            STT(graw[:], tmp1[:], -1.0, eA[:], ALU.mult, ALU.mult, ["g_tmp1", "g_eA"], ["g_graw"])
            MM(psum[:, 1, 0:256], Tri[:], fl(graw), ["Tri", "g_graw"], PS(1))
            MM(psum[:, 2, 0:256], onesF[:], fl(graw), ["onesF", "g_graw"], PS(2))
            ACTF(fl(cg), psum[:, 1, 0:256], AF.Copy, PS(1), ["g_cg"])
            ACTF(fl(gl), psum[:, 2, 0:256], AF.Copy, PS(2), ["g_gl"])
            ACTF(fl(eg), psum[:, 1, 0:256], AF.Exp, PS(1), ["g_eg"])
            ACTF(fl(sdec), psum[:, 2, 0:256], AF.Exp, PS(2), ["g_sdec"])
            TT("dve", fl(gl), fl(gl), fl(cg), ALU.subtract, ["g_gl", "g_cg"], ["g_gl"])
            ACTF(fl(kd), fl(gl), AF.Exp, ["g_gl"], ["g_kd"])
            TT("dve", fl(bgam), fl(beta), fl(eg), ALU.mult, ["g_beta", "g_eg"], ["g_bgam"])

            wrot = Rot("g_w", [sb(f"g_w{i}", [128, 8, 128], BF16) for i in range(3)])
            wz = [sb(f"g_wz{i}", [128, 8, 128], BF16) for i in range(2)]
            craw = sb("g_craw", [128, 3 + T])
            P.add("pool", lambda e: e.memset(craw[:, 0:3], 0.0), writes=["craw_pad"])
            xh = lambda k, i: xbufs[k][:, i * 512:(i + 1) * 512]
            t1 = Rot("xsA0", [xh(0, 0)])
            s1 = Rot("xsA0b", [xh(0, 1)])
            s1.next = lambda: (xh(0, 1), ("xsA0", 1))
            o_za = (xh(1, 0), ("xsA1", 0))
            o_oa = (xh(1, 1), ("xsA1", 1))
            o_qa = (wn[:, 0:512], "wn")
            o_ra = (wn[:, 512:1024], "wn")
            sq = Rot("g_sq", [sb(f"g_sq{i}", [128, 512]) for i in range(1)])
            rn = Rot("g_rn", [sb(f"g_rn{i}", [128, 512]) for i in range(1)])
            qT = sb("g_qT", [128, T], BF16)
            kT = sb("g_kT", [128, T], BF16)
            vT = [sb(f"g_vT{i}", [128, T], BF16) for i in range(2)]
            pj_bank = Rot("pjb", [0, 1])
            NI = 4
            b4 = lambda n: sb(n, [128, NI, 128], BF16)
            f4 = lambda n: sb(n, [128, NI, 128])
            ndL = f4("g_ndL")
            ndU = f4("g_ndU")
            grow = f4("g_grow")
            Abuf = b4("g_A")
            Msb = b4("g_Msb")
            kbg = b4("g_kbg")
            vb = b4("g_vb")
            qkmT = Rot("g_qkmT", [b4(f"g_qkmT{i}") for i in range(2)])
            qdT = Rot("g_qdT", [b4(f"g_qdT{i}") for i in range(2)])
            kdec = Rot("g_kdec", [b4(f"g_kdec{i}") for i in range(2)])
            u_sb = Rot("g_u", [f4(f"g_u{i}") for i in range(2)])
            wT_sb = Rot("g_wT", [b4(f"g_wT{i}") for i in range(2)])
            vnew = Rot("g_vnew", [sb(f"g_vnew{i}", [128, 2, 128], BF16) for i in range(2)])
            S = sb("g_S", [128, 2, 128])
            Sbf = sb("g_Sbf", [128, 2, 128], BF16)
            pT1 = pbank(1).bitcast(BF16)
            QSCALE = 128.0 ** -0.5
            print("gdn sbuf bytes remaining", nc.sbuf_bytes_remaining)

            def proj_tile(col0, kind, dest, dres):
                j = col0 // 128
                wa, wr = wrot.next()
                P.add("pool", lambda e: e.dma_start(out=wa[:], in_=gsrc[:, :, col0:col0 + 128]), writes=[wr], dma=True)
                for nb in range(4):
                    c0 = nb * 512
                    hres = [("hT", 4 * nb + i) for i in range(4)]
                    bk, _ = pj_bank.next()
                    for kc in range(8):
                        MM(pbank(bk), wa[:, kc, :], hT[:, kc, c0:c0 + 512], [wr] + hres, PS(bk),
                           start=(kc == 0), stop=(kc == 7))
                    ACTF(craw[:, 3 + c0:3 + c0 + 512], pbank(bk), AF.Copy, PS(bk), [("craw", nb)])
                    yield
                    halo = [("craw", nb - 1)] if nb > 0 else ["craw_pad"]
                    ta, tr = t1.next()
                    TS("pool", ta, craw[:, c0:c0 + 512], gcw[:, 0, j:j + 1], 0.0, ALU.mult, ALU.add,
                       [("craw", nb)] + halo + gcwr, [tr])
                    for k in (1, 2, 3):
                        STT(ta, craw[:, c0 + k:c0 + k + 512], gcw[:, k, j:j + 1], ta, ALU.mult, ALU.add,
                            [("craw", nb), tr] + halo + gcwr, [tr])
                    yield
                    if kind == "v":
                        ACTF(dest[:, c0:c0 + 512], ta, AF.Silu, [tr], [dres + (nb,)])
                        continue
                    sa, sr = s1.next()
                    ACTF(sa, ta, AF.Silu, [tr], [sr])
                    qa, qr = sq.next()
                    ACTF(qa[:], sa, AF.Square, [sr], [qr])
                    bk2, _ = pj_bank.next()
                    yield
                    MM(pbank(bk2), onesF[:], qa[:], ["onesF", qr], PS(bk2))
                    ra, rr = rn.next()
                    ACTF(ra[:], pbank(bk2), AF.Ln, PS(bk2), [rr], bias=EPS)
                    ACTF(ra[:], ra[:], AF.Exp, [rr], [rr], scale=-0.5)
                    yield
                    STT(dest[:, c0:c0 + 512], sa, (QSCALE if kind == "q" else 1.0), ra[:], ALU.mult, ALU.mult,
                        [sr, rr], [dres + (nb,)])

            def bufset(p):
                if p % 2 == 1 and p < 6:
                    aps = [yT[:, 12 + k, :] for k in range(4)]
                    res = [("yT", 12 + k) for k in range(4)]
                else:
                    aps = [qT[:], kT[:], vT[0][:], vT[1][:]]
                    res = [("g_qT",), ("g_kT",), ("g_vT0",), ("g_vT1",)]
                return aps, res

            def proj_gen(p):
                aps, res = bufset(p)
                yield from proj_tile(p * 128, "q", aps[0], res[0])
                yield from proj_tile(1024 + p * 128, "k", aps[1], res[1])
                for hh in range(2):
                    yield from proj_tile(2048 + (2 * p + hh) * 128, "v", aps[2 + hh], res[2 + hh])

            def drive(gens):
                gens = list(gens)
                while gens:
                    for g in list(gens):
                        try:
                            next(g)
                        except StopIteration:
                            gens.remove(g)

            v3 = lambda ap: ap.rearrange("p (a b) -> p a b", b=128)
            live01 = [0]

            def drive_bg(gens, bg, ratio=6):
                gens = list(gens)
                n = 0
                while gens:
                    for g in list(gens):
                        try:
                            next(g)
                        except StopIteration:
                            gens.remove(g)
                        n += 1
                        if bg[0] is not None and n % ratio == 0 and live01[0] == 0:
                            try:
                                next(bg[0])
                            except StopIteration:
                                bg[0] = None

            overlap_ok = lambda pn: pn < DBG_NPAIRS and bufset(pn)[1] != bufset(pn - 1)[1]
            drive([proj_gen(0)])
            for p in range(DBG_NPAIRS):
                hs = (2 * p, 2 * p + 1)
                (qA, kA, v0A, v1A), (qN, kN, v0N, v1N) = bufset(p)
                vA, vN = (v0A, v1A), (v0N, v1N)
                for hh in range(2):
                    zsrc = gsrc[:, :, 4096 + hs[hh] * 128:4096 + (hs[hh] + 1) * 128]
                    P.add("pool", lambda e, hh=hh, zsrc=zsrc: e.dma_start(out=wz[hh][:], in_=zsrc),
                          writes=[("g_wz", hh)], dma=True)
                bg = [proj_gen(p + 1) if overlap_ok(p + 1) else None]
                P.add("pool", lambda e: e.memset(S[:], 0.0), writes=[("g_S", 0), ("g_S", 1)])
                P.add("pool", lambda e: e.memset(Sbf[:], 0.0), writes=["g_Sbf"])
                st_ = {}
                if DBG_STAGE < 2:
                    continue

                def prep2(cp):
                    insts = [(cb, hh) for cb in range(2) for hh in range(2)]
                    live01[0] += 1
                    for cb in range(2):
                        c = 2 * cp + cb
                        cs = slice(c * 128, (c + 1) * 128)
                        nb = c // 4
                        qres, kres = qN + (nb,), kN + (nb,)
                        MM(psum[:, 0, cb * 256:cb * 256 + 128], kA[:, cs], kA[:, cs], [kres], PS(0))
                        MM(psum[:, 0, cb * 256 + 128:cb * 256 + 256], kA[:, cs], qA[:, cs], [kres, qres], PS(0))
                        TR(pT1[:, cb * 128:(cb + 1) * 128], kA[:, cs], [kres], PS(1))
                        for hh in range(2):
                            i = cb * 2 + hh
                            TR(pT1[:, 256 + i * 128:256 + (i + 1) * 128], vA[hh][:, cs], [vN[hh] + (nb,)], PS(1))
                    for i, (cb, hh) in enumerate(insts):
                        c = 2 * cp + cb
                        TS("pool", GR[:, i, :], onesF[:], graw[:, c, hs[hh]:hs[hh] + 1], 0.0, ALU.mult, ALU.add,
                           ["onesF", "g_graw"], [("g_GR", i)])
                        MM(psum[:, 2, i * 128:(i + 1) * 128], GR[:, i, :], Tri[:], [("g_GR", i), "Tri"], PS(2))
                    yield
                    for i, (cb, hh) in enumerate(insts):
                        c, h = 2 * cp + cb, hs[hh]
                        gb = psum[:, 2, i * 128:(i + 1) * 128]
                        STT(ndL[:, i, :], gb, cg[:, c, h:h + 1], maskL[:], ALU.subtract, ALU.max,
                            PS(2) + ["g_cg", "maskL"], [("g_ndL", i)])
                        STT(ndU[:, i, :], gb, cg[:, c, h:h + 1], maskU[:], ALU.subtract, ALU.min,
                            PS(2) + ["g_cg", "maskU"], [("g_ndU", i)])
                    allr = lambda n: [(n, i) for i in range(NI)]
                    ACTF(grow[:], v3(pbank(2)), AF.Exp, PS(2), allr("g_grow"))
                    ACTF(ndL[:], ndL[:], AF.Exp, allr("g_ndL"), allr("g_ndL"), scale=-1.0)
                    ACTF(ndU[:], ndU[:], AF.Exp, allr("g_ndU"), allr("g_ndU"))
                    yield
                    qk_a, qk_r = qkmT.next()
                    qd_a, qd_r = qdT.next()
                    kd_a, kd_r = kdec.next()
                    for i, (cb, hh) in enumerate(insts):
                        c, h = 2 * cp + cb, hs[hh]
                        cs = slice(c * 128, (c + 1) * 128)
                        STT(Abuf[:, i, :], psum[:, 0, cb * 256:cb * 256 + 128], beta[:, c, h:h + 1], ndL[:, i, :],
                            ALU.mult, ALU.mult, PS(0) + ["g_beta", ("g_ndL", i)], [("g_A", i)])
                        TT("dve", qk_a[:, i, :], psum[:, 0, cb * 256 + 128:cb * 256 + 256], ndU[:, i, :], ALU.mult,
                           PS(0) + [("g_ndU", i)], [(qk_r, i)])
                        TT("pool", qd_a[:, i, :], qA[:, cs], grow[:, i, :], ALU.mult,
                           [qN + (c // 4,), ("g_grow", i)], [(qd_r, i)])
                        AMUL(kbg[:, i, :], pT1[:, cb * 128:(cb + 1) * 128], bgam[:, c, h:h + 1],
                             PS(1) + ["g_bgam"], [("g_kbg", i)])
                        AMUL(kd_a[:, i, :], pT1[:, cb * 128:(cb + 1) * 128], kd[:, c, h:h + 1],
                             PS(1) + ["g_kd"], [(kd_r, i)])
                        AMUL(vb[:, i, :], pT1[:, 256 + i * 128:256 + (i + 1) * 128], beta[:, c, h:h + 1],
                             PS(1) + ["g_beta"], [("g_vb", i)])
                    live01[0] -= 1
                    yield
                    cur, curr = UT0, [["UT0"], ["UT0"]]
                    p4 = [psum[:, 4, :].rearrange("p (a b c) -> p a b c", a=2, b=2),
                          psum[:, 7, :].rearrange("p (a b c) -> p a b c", a=2, b=2)]
                    mbank = (3, 2)
                    pbk = (4, 7)
                    for lv in range(7):
                        for half in range(2):
                            for k in range(2):
                                i = 2 * half + k
                                MM(psum[:, mbank[half], k * 128:(k + 1) * 128], Abuf[:, i, :], cur[:, i, 0, :],
                                   [("g_A", i)] + curr[half], PS(mbank[half]))
                        yield
                        for half in range(2):
                            TT("dve", Msb[:, 2 * half:2 * half + 2, :], v3(psum[:, mbank[half], 0:256]),
                               maskB[:, lv, 2 * half:2 * half + 2, :], ALU.mult,
                               PS(mbank[half]) + [("maskB", lv)], [("g_Msb", half)])
                            for k in range(2):
                                i = 2 * half + k
                                pp = p4[half]
                                MM(pp[:, k, 0, :], cur[:, i, 1, :], Msb[:, i, :], curr[half] + [("g_Msb", half)],
                                   PS(pbk[half]))
                                if lv < 6:
                                    MM(pp[:, k, 1, :], Msb[:, i, :], cur[:, i, 1, :], curr[half] + [("g_Msb", half)],
                                       PS(pbk[half]))
                        yield
                        nxt = UTw[lv % 2]
                        nxtr = [[("UTw", lv % 2, 0)], [("UTw", lv % 2, 1)]]
                        for half in range(2):
                            STT(nxt[:, 2 * half:2 * half + 2].rearrange("p a b c -> p (a b c)"),
                                pbank(pbk[half]), -1.0,
                                cur[:, 2 * half:2 * half + 2].rearrange("p a b c -> p (a b c)"),
                                ALU.mult, ALU.add, curr[half] + PS(pbk[half]), nxtr[half])
                        cur, curr = nxt, nxtr
                    curr = curr[0] + curr[1]
                    for i in range(NI):
                        MM(psum[:, 2, i * 128:(i + 1) * 128], cur[:, i, 0, :], vb[:, i, :], curr + [("g_vb", i)], PS(2))
                    for i in range(NI):
                        MM(psum[:, 3, i * 128:(i + 1) * 128], kbg[:, i, :], cur[:, i, 0, :], curr + [("g_kbg", i)], PS(3))
                    yield
                    u_a, u_r = u_sb.next()
                    w_a, w_r = wT_sb.next()
                    ACTF(u_a[:], v3(pbank(2)), AF.Copy, PS(2), [u_r])
                    ACTF(w_a[:], v3(pbank(3)), AF.Copy, PS(3), [w_r])
                    st_[cp] = dict(u=(u_a, u_r), w=(w_a, w_r), qk=(qk_a, qk_r), qd=(qd_a, qd_r), kd=(kd_a, kd_r))

                def recur(c):
                    d = st_[c // 2]
                    (u_a, u_r), (w_a, w_r), (qk_a, qk_r), (qd_a, qd_r), (kd_a, kd_r) = d["u"], d["w"], d["qk"], d["qd"], d["kd"]
                    cb = c % 2
                    i0_ = cb * 2
                    for hh in range(2):
                        MM(psum[:, 5, hh * 128:(hh + 1) * 128], w_a[:, i0_ + hh, :], Sbf[:, hh, :], [w_r, "g_Sbf"], PS(5))
                    yield
                    vn_a, vn_r = vnew.next()
                    STT(vn_a[:].rearrange("p a b -> p (a b)"), psum[:, 5, 0:256], -1.0,
                        u_a[:, i0_:i0_ + 2, :].rearrange("p a b -> p (a b)"), ALU.mult, ALU.add, [u_r] + PS(5), [vn_r])
                    for hh in range(2):
                        ocol = hh * 256 + cb * 128
                        MM(psum[:, 6, ocol:ocol + 128], Sbf[:, hh, :], qd_a[:, i0_ + hh, :], ["g_Sbf", (qd_r, i0_ + hh)],
                           PS(6), start=True, stop=False)
                        MM(psum[:, 6, ocol:ocol + 128], vn_a[:, hh, :], qk_a[:, i0_ + hh, :], [vn_r, (qk_r, i0_ + hh)],
                           PS(6), start=False, stop=True)
                        MM(psum[:, 5, 256 + hh * 128:256 + (hh + 1) * 128], kd_a[:, i0_ + hh, :], vn_a[:, hh, :],
                           [(kd_r, i0_ + hh), vn_r], PS(5))
                    yield
                    for hh in range(2):
                        STT(S[:, hh, :], S[:, hh, :], sdec[:, c, hs[hh]:hs[hh] + 1],
                            psum[:, 5, 256 + hh * 128:256 + (hh + 1) * 128], ALU.mult, ALU.add,
                            [("g_S", hh), "g_sdec"] + PS(5), [("g_S", hh)])
                    ACTF(Sbf[:], S[:], AF.Copy, [("g_S", 0), ("g_S", 1)], ["g_Sbf"])
                    if cb == 1:
                        t0_ = (c - 1) * 128
                        hres = [("hT", c - 1), ("hT", c)]
                        yield
                        live01[0] += 1
                        for hh in range(2):
                            for kc in range(8):
                                MM(psum[:, 0, hh * 256:(hh + 1) * 256], wz[hh][:, kc, :], hT[:, kc, t0_:t0_ + 256],
                                   [("g_wz", hh)] + hres, PS(0), start=(kc == 0), stop=(kc == 7))
                        qa, qr = o_qa
                        ACTF(qa, pbank(6), AF.Square, PS(6), [qr])
                        yield
                        MM(pbank(1), onesF[:], qa, ["onesF", qr], PS(1))
                        za, zr = o_za
                        ACTF(za, pbank(0), AF.Silu, PS(0), [zr])
                        yield
                        ra, rr = o_ra
                        ACTF(ra, pbank(1), AF.Ln, PS(1), [rr], scale=1.0 / 128, bias=EPS)
                        live01[0] -= 1
                        yield
                        ACTF(ra, ra, AF.Exp, [rr], [rr], scale=-0.5)
                        yield
                        oa, orr = o_oa
                        STT(oa, pbank(6), gnw[:, 0:1], ra, ALU.mult, ALU.mult, PS(6) + ["gnw", rr], [orr])
                        yield
                        TT("pool", yT[:, hs[0]:hs[0] + 2, t0_:t0_ + 256], oa.rearrange("p (a b) -> p a b", a=2),
                           za.rearrange("p (a b) -> p a b", a=2), ALU.mult, [orr, zr],
                           [("yT", hs[0], c // 4), ("yT", hs[1], c // 4)])

                def tail(cp):
                    yield from recur(2 * cp)
                    yield
                    yield from recur(2 * cp + 1)

                ncp = DBG_NCHUNK // 2
                drive_bg([prep2(0)], bg)
                for cp in range(ncp):
                    gens = [tail(cp)]
                    if cp + 1 < ncp:
                        gens.insert(0, prep2(cp + 1))
                    drive_bg(gens, bg)
                if bg[0] is not None:
                    drive([bg[0]])
                elif p + 1 < DBG_NPAIRS and not overlap_ok(p + 1):
                    drive([proj_gen(p + 1)])

        def load_xs(xs):
            for t in range(NT):
                P.add("sp", lambda e, t=t: e.dma_start(out=xs[:, t, :], in_=x_d[t * 128:(t + 1) * 128, :]),
                      writes=[("xs", t)], dma=True)

        xs = None
        scr = None
        for li, l in enumerate(layers):
            final = li == len(layers) - 1
            if l == 0:
                with ExitStack() as s0:
                    sb0 = mk_sb(s0)
                    xbufs = [sb0(f"xstream{i}", [128, D]) for i in range(2)]

                    def xsrc(t):
                        k = t % 2
                        P.add("sp", lambda e, t=t, k=k: e.dma_start(out=xbufs[k][:], in_=x_d[t * 128:(t + 1) * 128, :]),
                              writes=[(f"xsA{k}", 0), (f"xsA{k}", 1)], dma=True)
                        return xbufs[k][:], [(f"xsA{k}", 0), (f"xsA{k}", 1)]
                    phase_norm(0, xsrc)
                    phase_gdn(sb0, xbufs)
                P.barrier()
                xs = sb("xs", [128, NT, D])
                scr = (sb("scrA", [128, D]), sb("scrB", [128, D]))
                load_xs(xs)
                phase_out(0, gwout_d, final, xs, sb, *scr)
            else:
                if xs is None:
                    xs = sb("xs", [128, NT, D])
                    scr = (sb("scrA", [128, D]), sb("scrB", [128, D]))
                    load_xs(xs)
                phase_norm(1, lambda t: (xs[:, t, :], [("xs", t)]), stats_first=True)
                phase_sc(sb, *scr)
                phase_out(1, swout_d, final, xs, sb, *scr)
        P.emit(st)
        global LAST_P
        LAST_P = P
    return nc


_NC_CACHE = {}


def _get_nc(layers):
    if layers not in _NC_CACHE:
        _NC_CACHE[layers] = build_program(layers)
    return _NC_CACHE[layers]


def _in_maps(inputs, xin):
    f = lambda a: np.ascontiguousarray(np.asarray(a, dtype=np.float32))
    shared = {
        "pre_norm_w": f(inputs["pre_norm_w"]),
        "post_norm_w": f(inputs["post_norm_w"]),
        "gdn_w_in": f(inputs["gdn_w_in"][0]),
        "gdn_conv_w": f(inputs["gdn_conv_w"][0]),
        "gdn_A_log": f(inputs["gdn_A_log"]).reshape(1, 16),
        "gdn_dt_bias": f(inputs["gdn_dt_bias"]).reshape(1, 16),
        "gdn_norm_w": f(inputs["gdn_norm_w"]).reshape(128, 1),
        "gdn_w_out": f(inputs["gdn_w_out"][0]),
        "sc_w_in": f(inputs["sc_w_in"][0]),
        "sc_conv_w": f(inputs["sc_conv_w"][0]),
        "sc_w_out": f(inputs["sc_w_out"][0]),
    }
    return [dict(shared, x=f(xin[b])) for b in range(8)]


def run_layers(inputs, layers, xin=None):
    x = np.asarray(inputs["x"], dtype=np.float32) if xin is None else xin
    nc = _get_nc(tuple(layers))
    res = run_bass_kernel_spmd(nc, _in_maps(inputs, x), core_ids=list(range(8)))
    return np.stack([np.asarray(r["out"], dtype=np.float32) for r in res.results], axis=0)


FUSED = True


def kernel(**inputs):
    if FUSED:
        return run_layers(inputs, (0, 1))
    x1 = run_layers(inputs, (0,))
    return run_layers(inputs, (1,), xin=x1)
```

```python
from contextlib import ExitStack

import numpy as np
import concourse.bass as bass
import concourse.mybir as mybir
from concourse.bass_utils import run_bass_kernel_spmd

F32 = mybir.dt.float32
BF16 = mybir.dt.bfloat16
ALU = mybir.AluOpType
AF = mybir.ActivationFunctionType

T = 2048
D = 1024
NT = T // 128
EPS = 1e-6
GDN_PROJ = 6176

ENGS = ("pe", "act", "dve", "pool", "sp")
DBG_NPAIRS = 8
DBG_NCHUNK = 16
DBG_STAGE = 9
DBG_PREP_STEPS = 0
DBG_PREP2_STEPS = 0
DBG_ROT1 = 0
DBG_SKIP = 0
N_DMA_SEMS = 40


class Op:
    __slots__ = ("eng", "fn", "deps", "signal", "cnt", "is_dma", "dslot", "dval", "is_out")


class Prog:
    def __init__(self, nc):
        self.nc = nc
        self.ops = {e: [] for e in ENGS}
        self.res = {}
        self.dma_rr = {"sp": 0, "pool": 0, "act": 0}
        self.dma_last = [None] * N_DMA_SEMS
        self.dma_val = [0] * N_DMA_SEMS
        self.out_dmas = []
        self.bar_deps = []
        self.bar_pending = set()

    def barrier(self):
        deps = [self.ops[e][-1] for e in ENGS if self.ops[e] and not self.ops[e][-1].is_dma]
        for e in ENGS:
            for o in reversed(self.ops[e]):
                if not o.is_dma:
                    if o not in deps:
                        deps.append(o)
                    break
        deps += [d for d in self.dma_last if d is not None]
        self.bar_deps = deps
        self.bar_pending = set(ENGS)

    def add(self, eng, fn, reads=(), writes=(), dma=False, is_out=False):
        op = Op()
        op.eng, op.fn, op.is_dma, op.signal, op.cnt, op.is_out = eng, fn, dma, False, 0, is_out
        deps = []
        if eng in self.bar_pending:
            self.bar_pending.discard(eng)
            deps.extend(d for d in self.bar_deps if d.is_dma or d.eng != eng or dma)

        def need(d, raw):
            if d is op:
                return
            if d.is_dma:
                deps.append(d)
            elif d.eng == eng and not dma:
                if raw and eng != "pe":
                    deps.append(d)
            else:
                deps.append(d)

        excl = [r for r in reads if isinstance(r, tuple) and r[0] == "ps"]
        if excl:
            reads = [r for r in reads if r not in excl]
            for r in excl:
                st = self.res.get(r)
                if st:
                    for d in st[0]:
                        need(d, True)
            writes = list(writes) + excl
        for r in reads:
            st = self.res.get(r)
            if st:
                for d in st[0]:
                    need(d, True)
        for r in writes:
            st = self.res.get(r)
            if st:
                for d in st[0]:
                    need(d, False)
                for d in st[1].values():
                    need(d, False)
                for d in st[2]:
                    need(d, False)
        for r in reads:
            st = self.res.setdefault(r, [[], {}, []])
            if dma:
                st[2].append(op)
            else:
                st[1][eng] = op
        for r in writes:
            self.res[r] = [[op], {}, []]
        if dma:
            half = N_DMA_SEMS // 2
            base = 0 if eng == "sp" else half
            slot = base + self.dma_rr[eng]
            self.dma_rr[eng] = (self.dma_rr[eng] + 1) % half
            prev = self.dma_last[slot]
            if prev is not None:
                deps.append(prev)
            self.dma_val[slot] += 16
            op.dslot, op.dval = slot, self.dma_val[slot]
            self.dma_last[slot] = op
            if is_out:
                self.out_dmas.append(op)
        op.deps = deps
        for d in deps:
            if not d.is_dma:
                d.signal = True
        self.ops[eng].append(op)
        return op

    def emit(self, stack):
        nc = self.nc
        esem = {e: stack.enter_context(nc.semaphore(f"s_{e}")) for e in ENGS}
        dsem = [stack.enter_context(nc.semaphore(f"s_d{i}")) for i in range(N_DMA_SEMS)]
        for e in ENGS:
            c = 0
            for op in self.ops[e]:
                if op.signal and not op.is_dma:
                    c += 1
                    op.cnt = c
        block = stack.enter_context(nc.Block())
        outs = self.out_dmas

        def run(e, engine):
            waited = {}

            def w(sem, key, val):
                if waited.get(key, 0) >= val:
                    return
                waited[key] = val
                engine.wait_ge(sem, val)

            for op in self.ops[e]:
                for d in op.deps:
                    if d.is_dma:
                        w(dsem[d.dslot], ("d", d.dslot), d.dval)
                    else:
                        w(esem[d.eng], d.eng, d.cnt)
                ins = op.fn(engine)
                if op.is_dma:
                    ins.then_inc(dsem[op.dslot], 16)
                elif op.signal:
                    ins.then_inc(esem[e], 1)
            if e == "sp":
                for d in outs:
                    w(dsem[d.dslot], ("d", d.dslot), d.dval)

        @block.tensor
        def _(engine):
            run("pe", engine)

        @block.scalar
        def _(engine):
            run("act", engine)

        @block.vector
        def _(engine):
            run("dve", engine)

        @block.gpsimd
        def _(engine):
            run("pool", engine)

        @block.sync
        def _(engine):
            run("sp", engine)


class Rot:
    def __init__(self, name, aps):
        self.name, self.aps, self.i = name, aps, 0

    def next(self):
        k = self.i % (1 if DBG_ROT1 else len(self.aps))
        self.i += 1
        return self.aps[k], (self.name, k)


def build_program(layers=(0, 1)):
    nc = bass.Bass("TRN2", target_bir_lowering=False)
    dt = lambda n, s, kind="ExternalInput": nc.dram_tensor(n, s, F32, kind=kind).ap()
    x_d = dt("x", [T, D])
    out_d = dt("out", [T, D], "ExternalOutput")
    pre_d = dt("pre_norm_w", [2, D])
    post_d = dt("post_norm_w", [2, D])
    gwin_d = dt("gdn_w_in", [D, GDN_PROJ])
    gcw_d = dt("gdn_conv_w", [4, 4096])
    galog_d = dt("gdn_A_log", [1, 16])
    gdtb_d = dt("gdn_dt_bias", [1, 16])
    gnw_d = dt("gdn_norm_w", [128, 1])
    gwout_d = dt("gdn_w_out", [2048, D])
    swin_d = dt("sc_w_in", [D, 8192])
    scw_d = dt("sc_conv_w", [3, 2048])
    swout_d = dt("sc_w_out", [2048, D])

    with ExitStack() as st:
        def mk_sb(stack):
            def sb(name, shape, dty=F32):
                return stack.enter_context(nc.sbuf_tensor(name, shape, dty))
            return sb
        sb = mk_sb(st)

        P = Prog(nc)
        hT = sb("hT", [128, 8, T], BF16)
        wout = hT[:].rearrange("p a b -> p (a b)").rearrange("p (j c) -> p j c", c=D)
        yT = sb("yT", [128, 16, T], BF16)
        ident = sb("ident", [128, 128], BF16)
        wn = sb("wn", [128, D])
        junk = sb("junk", [128, D], BF16)
        ss = sb("ss", [128, NT])
        sd = sb("sd", [128, NT])
        rstd = sb("rstd", [128, NT])
        hn = [sb(f"hn{i}", [128, D], BF16) for i in range(2)]
        psum = st.enter_context(nc.psum_tensor("psum", [128, 8, 512], F32))
        pbank = lambda b: psum[:, b, :]
        hT_all = [("hT", t) for t in range(NT)]
        BANKRES = {b: [("ps", b)] for b in range(8)}
        PS = lambda b: BANKRES[b]

        def MM(out, lhsT, rhs, reads, writes, start=True, stop=True):
            return P.add("pe", lambda e: e.matmul(out, lhsT=lhsT, rhs=rhs, start=start, stop=stop), reads, writes)

        def TR(out, in_, reads, writes):
            return P.add("pe", lambda e: e.transpose(out=out, in_=in_, identity=ident[:]), list(reads) + ["ident"], writes)

        def ACTF(out, in_, func, reads, writes, **kw):
            return P.add("act", lambda e: e.activation(out=out, in_=in_, func=func, **kw), reads, writes)

        def AMUL(out, in_, mul, reads, writes):
            return P.add("act", lambda e: e.mul(out, in_, mul), reads, writes)

        def TT(eng, out, in0, in1, op, reads, writes):
            return P.add(eng, lambda e: e.tensor_tensor(out=out, in0=in0, in1=in1, op=op), reads, writes)

        def STT(out, in0, scalar, in1, op0, op1, reads, writes):
            return P.add("dve", lambda e: e.scalar_tensor_tensor(out=out, in0=in0, scalar=scalar, in1=in1,
                                                                 op0=op0, op1=op1), reads, writes)

        def TS(eng, out, in0, s1, s2, op0, op1, reads, writes):
            return P.add(eng, lambda e: e.tensor_scalar(out=out, in0=in0, scalar1=s1, scalar2=s2, op0=op0, op1=op1),
                         reads, writes)

        P.add("pool", lambda e: e.memset(ident[:], 1.0), writes=["ident"])
        P.add("pool", lambda e: e.affine_select(out=ident[:], in_=ident[:], pattern=[[-1, 128]],
                                                compare_op=ALU.is_equal, fill=0.0, base=0,
                                                channel_multiplier=1), reads=["ident"], writes=["ident"])

        def phase_norm(l, xsrc, stats_first=False):
            P.add("sp", lambda e: e.dma_start(out=wn[:], in_=pre_d[l].partition_broadcast(128)),
                  writes=["wn"], dma=True)

            def stats(t, xa, xr):
                P.add("act", lambda e, t=t, xa=xa: e.activation(out=junk[:], in_=xa, func=AF.Square,
                                                                accum_out=ss[:, t:t + 1]),
                      reads=xr, writes=["junk", ("ss", t)])
                P.add("act", lambda e, t=t: e.activation(out=sd[:, t:t + 1], in_=ss[:, t:t + 1], func=AF.Sqrt,
                                                         scale=1.0 / D, bias=EPS),
                      reads=[("ss", t)], writes=[("sd", t)])
                P.add("dve", lambda e, t=t: e.reciprocal(out=rstd[:, t:t + 1], in_=sd[:, t:t + 1]),
                      reads=[("sd", t)], writes=[("rstd", t)])

            srcs = {}
            if stats_first:
                for t in range(NT):
                    srcs[t] = xsrc(t)
                    stats(t, *srcs[t])
            for t in range(NT):
                if t not in srcs:
                    srcs[t] = xsrc(t)
                    stats(t, *srcs[t])
                xa, xr = srcs[t]
                h = hn[t % 2]
                hr = ("hn", t % 2)
                P.add("dve", lambda e, t=t, h=h, xa=xa: e.scalar_tensor_tensor(
                    out=h[:], in0=xa, scalar=rstd[:, t:t + 1], in1=wn[:],
                    op0=ALU.mult, op1=ALU.mult), reads=xr + [("rstd", t), "wn"], writes=[hr])
                b = t % 2
                pT = pbank(b).bitcast(BF16)
                for dc in range(8):
                    P.add("pe", lambda e, h=h, dc=dc, pT=pT: e.transpose(
                        out=pT[:, dc * 128:(dc + 1) * 128], in_=h[:, dc * 128:(dc + 1) * 128], identity=ident[:]),
                        reads=[hr, "ident"], writes=PS(b))
                P.add("act", lambda e, t=t, pT=pT: e.copy(
                    out=hT[:, :, t * 128:(t + 1) * 128], in_=pT.rearrange("p (a b) -> p a b", b=128)),
                    reads=PS(b), writes=[("hT", t)])

        def phase_sc(sb, scrA, scrB):
            cw = sb("sc_cw", [128, 3, 16])
            for k in range(3):
                P.add("sp", lambda e, k=k: e.dma_start(out=cw[:, k, :], in_=scw_d[k].rearrange("(j p) -> p j", p=128),
                                                       allow_slow_non_contiguous=True),
                      writes=[("sc_cw", k)], dma=True)
            wrot = Rot("sc_w", [sb(f"sc_w{i}", [128, 8, 128], BF16) for i in range(5)])
            cu = sb("sc_cu", [128, 2 + T])
            u_sb = Rot("scrA", [scrA[:, 0:512], scrA[:, 512:1024]])
            sz = Rot("scrB", [scrB[:, 0:512], scrB[:, 512:1024]])
            t1 = Rot("sc_t1", [sb(f"sc_t1{i}", [128, 512]) for i in range(2)])
            gg = Rot("sc_g", [sb(f"sc_g{i}", [128, 512]) for i in range(2)])
            P.add("pool", lambda e: e.memset(cu[:, 0:2], 0.0), writes=["cu_pad"])
            src = swin_d.rearrange("(kc p) (ty j c) -> p kc ty j c", p=128, ty=4, c=128)
            for j in range(16):
                wts = []
                for ty in range(4):
                    wa, wr = wrot.next()
                    P.add("pool", lambda e, wa=wa, j=j, ty=ty: e.dma_start(out=wa[:], in_=src[:, :, ty, j, :]),
                          writes=[wr], dma=True)
                    wts.append((wa, wr))
                for nb in range(4):
                    c0 = nb * 512
                    hres = [("hT", 4 * nb + i) for i in range(4)]
                    bk = [(nb % 2) * 4 + ty for ty in range(4)]
                    for ty in range(4):
                        wa, wr = wts[ty]
                        for kc in range(8):
                            P.add("pe", lambda e, wa=wa, kc=kc, c0=c0, b=bk[ty]: e.matmul(
                                pbank(b), lhsT=wa[:, kc, :], rhs=hT[:, kc, c0:c0 + 512],
                                start=(kc == 0), stop=(kc == 7)),
                                reads=[wr] + hres, writes=PS(bk[ty]))
                    pu, pB, pC, pz = [pbank(b) for b in bk]
                    ru, rB, rC, rz = [PS(b) for b in bk]
                    ua, ur = u_sb.next()
                    P.add("act", lambda e, ua=ua, pu=pu: e.copy(out=ua, in_=pu), reads=ru, writes=[ur])
                    P.add("dve", lambda e, ua=ua, pC=pC, c0=c0: e.tensor_tensor(
                        out=cu[:, 2 + c0:2 + c0 + 512], in0=pC, in1=ua, op=ALU.mult),
                        reads=rC + [ur], writes=[("cu", nb)])
                    za, zr = sz.next()
                    P.add("act", lambda e, za=za, pz=pz: e.activation(out=za, in_=pz, func=AF.Silu),
                          reads=rz, writes=[zr])
                    ta, tr = t1.next()
                    halo = [("cu", nb - 1)] if nb > 0 else ["cu_pad"]
                    P.add("pool", lambda e, ta=ta, c0=c0, j=j: e.tensor_scalar(
                        out=ta[:], in0=cu[:, c0:c0 + 512], scalar1=cw[:, 0, j:j + 1], scalar2=0.0,
                        op0=ALU.mult, op1=ALU.add), reads=[("cu", nb), ("sc_cw", 0)] + halo, writes=[tr])
                    P.add("dve", lambda e, ta=ta, c0=c0, j=j: e.scalar_tensor_tensor(
                        out=ta[:], in0=cu[:, 1 + c0:1 + c0 + 512], scalar=cw[:, 1, j:j + 1], in1=ta[:],
                        op0=ALU.mult, op1=ALU.add), reads=[("cu", nb), ("sc_cw", 1), tr] + halo, writes=[tr])
                    P.add("dve", lambda e, ta=ta, c0=c0, j=j: e.scalar_tensor_tensor(
                        out=ta[:], in0=cu[:, 2 + c0:2 + c0 + 512], scalar=cw[:, 2, j:j + 1], in1=ta[:],
                        op0=ALU.mult, op1=ALU.add), reads=[("cu", nb), ("sc_cw", 2), tr], writes=[tr])
                    ga, gr = gg.next()
                    P.add("dve", lambda e, ga=ga, pB=pB, za=za: e.tensor_tensor(
                        out=ga[:], in0=pB, in1=za, op=ALU.mult), reads=rB + [zr], writes=[gr])
                    P.add("pool", lambda e, ga=ga, ta=ta, j=j, c0=c0: e.tensor_tensor(
                        out=yT[:, j, c0:c0 + 512], in0=ta[:], in1=ga[:], op=ALU.mult),
                        reads=[tr, gr], writes=[("yT", j, nb)])

        def phase_out(l, wout_d, final, xs, sb, scrA, scrB):
            P.add("sp", lambda e: e.dma_start(out=wn[:], in_=post_d[l].partition_broadcast(128)),
                  writes=["wn"], dma=True)
            wsrc = wout_d.rearrange("(j p) c -> p j c", p=128)
            for h in range(16):
                P.add("pool", lambda e, h=h: e.dma_start(out=wout[:, h:h + 1, :], in_=wsrc[:, h:h + 1, :]),
                      writes=(hT_all if h == 0 else []) + [("wout", h)], dma=True)
            ssq = sb(f"ssq{l}", [128, NT])
            sdq = sb(f"sdq{l}", [128, NT])
            rsq = sb(f"rsq{l}", [128, NT])
            tmps = [(scrA, [("scrA", 0), ("scrA", 1)]), (scrB, [("scrB", 0), ("scrB", 1)])]
            for t in range(NT):
                b0 = (t % 2) * 2
                for half in range(2):
                    for j in range(16):
                        P.add("pe", lambda e, t=t, half=half, j=j, b0=b0: e.matmul(
                            pbank(b0 + half), lhsT=yT[:, j, t * 128:(t + 1) * 128],
                            rhs=wout[:, j, half * 512:(half + 1) * 512], start=(j == 0), stop=(j == 15)),
                            reads=[("wout", j), ("wout", 0), ("yT", j, t // 4)] + (hT_all if j == 15 else []),
                            writes=PS(b0 + half))
                pp = psum[:, b0:b0 + 2, :].rearrange("p a b -> p (a b)")
                pr = PS(b0) + PS(b0 + 1)
                P.add("act", lambda e, t=t, pp=pp: e.activation(out=junk[:], in_=pp, func=AF.Square,
                                                                accum_out=ssq[:, t:t + 1]),
                      reads=pr, writes=["junk", ("ssq", t)])
                P.add("act", lambda e, t=t: e.activation(out=sdq[:, t:t + 1], in_=ssq[:, t:t + 1], func=AF.Sqrt,
                                                         scale=1.0 / D, bias=EPS),
                      reads=[("ssq", t)], writes=[("sdq", t)])
                P.add("dve", lambda e, t=t: e.reciprocal(out=rsq[:, t:t + 1], in_=sdq[:, t:t + 1]),
                      reads=[("sdq", t)], writes=[("rsq", t)])
                ta, tr = tmps[t % 2]
                P.add("dve", lambda e, t=t, ta=ta, pp=pp: e.scalar_tensor_tensor(
                    out=ta[:], in0=pp, scalar=rsq[:, t:t + 1], in1=wn[:], op0=ALU.mult, op1=ALU.mult),
                    reads=pr + [("rsq", t), "wn"], writes=tr)
                P.add("pool", lambda e, t=t, ta=ta: e.tensor_tensor(
                    out=xs[:, t, :], in0=xs[:, t, :], in1=ta[:], op=ALU.add),
                    reads=[("xs", t)] + tr, writes=[("xs", t)])
                if final:
                    P.add("sp", lambda e, t=t: e.dma_start(out=out_d[t * 128:(t + 1) * 128, :], in_=xs[:, t, :]),
                          reads=[("xs", t)], dma=True, is_out=True)

        def phase_gdn(sb, xbufs):
            BIG = 1.0e4
            gsrc = gwin_d.rearrange("(kc p) n -> p kc n", p=128)
            onesF = sb("onesF", [128, 128])
            Tri = sb("Tri", [128, 128])
            maskL = sb("maskL", [128, 128])
            maskU = sb("maskU", [128, 128])
            GR = sb("g_GR", [128, 4, 128])
            Eb = GR[:].rearrange("p a b -> p (a b)").bitcast(BF16).rearrange("p (a b) -> p a b", b=128)
            maskB = sb("maskB", [128, 7, 4, 128], BF16)
            UT0 = sb("UT0", [128, 4, 2, 128], BF16)
            UTw = [sb(f"UTw{i}", [128, 4, 2, 128], BF16) for i in range(2)]
            zerosF = sb("zerosF", [128, 128])
            sel = lambda out, in_, pat, op, fill, base, cm, reads, writes: P.add(
                "pool", lambda e: e.affine_select(out=out, in_=in_, pattern=pat, compare_op=op, fill=fill,
                                                  base=base, channel_multiplier=cm), reads, writes)
            P.add("pool", lambda e: e.memset(onesF[:], 1.0), writes=["onesF"])
            P.add("pool", lambda e: e.memset(zerosF[:], 0.0), writes=["zerosF"])
            sel(Tri[:], onesF[:], [[1, 128]], ALU.is_ge, 0.0, 0, -1, ["onesF"], ["Tri"])
            sel(maskL[:], zerosF[:], [[-1, 128]], ALU.is_ge, BIG, 0, 1, ["zerosF"], ["maskL"])
            sel(maskU[:], zerosF[:], [[1, 128]], ALU.is_ge, -BIG, 0, -1, ["zerosF"], ["maskU"])
            P.add("pool", lambda e: e.tensor_copy(out=Eb[:, 0, :], in_=ident[:]), reads=["ident"], writes=[("Eb", 0)])
            P.add("pool", lambda e: e.memset(Eb[:, 7, :], 1.0), writes=[("Eb", 7)])
            for k in range(1, 7):
                b = 1 << k
                sel(Eb[:, k, :], onesF[:], [[-b, 128 // b], [0, b]], ALU.is_ge, 0.0, 0, 1, ["onesF"], [("Eb", k)])
                sel(Eb[:, k, :], Eb[:, k, :], [[b, 128 // b], [0, b]], ALU.is_ge, 0.0, b - 1, -1,
                    [("Eb", k)], [("Eb", k)])
            for lv in range(7):
                for i in range(4):
                    TT("pool", maskB[:, lv, i, :], Eb[:, lv + 1, :], Eb[:, lv, :], ALU.subtract,
                       [("Eb", lv + 1), ("Eb", lv)], [("maskB", lv)] if i == 3 else [("maskB_part", lv, i)])
            for a in range(4):
                for b_ in range(2):
                    P.add("pool", lambda e, a=a, b_=b_: e.tensor_copy(out=UT0[:, a, b_, :], in_=ident[:]),
                          reads=["ident"], writes=["UT0"] if (a, b_) == (3, 1) else [("UT0_part", a, b_)])

            gnw = sb("gnw", [128, 1])
            P.add("sp", lambda e: e.dma_start(out=gnw[:], in_=gnw_d), writes=["gnw"], dma=True)
            gcw = sb("gcw", [128, 4, 32])
            for k in range(4):
                P.add("sp", lambda e, k=k: e.dma_start(out=gcw[:, k, :], in_=gcw_d[k].rearrange("(j p) -> p j", p=128),
                                                       allow_slow_non_contiguous=True),
                      writes=[("gcw", k)], dma=True)
            gcwr = [("gcw", k) for k in range(4)]

            sm = lambda n: sb(n, [128, 16, 16])
            beta, tmp1, dtb, eA, graw, cg, gl, eg, bgam, kd, sdec = [sm(n) for n in (
                "g_beta", "g_tmp1", "g_dtb", "g_eA", "g_graw", "g_cg", "g_gl", "g_eg", "g_bgam", "g_kd", "g_sdec")]
            fl = lambda a: a[:].rearrange("p a b -> p (a b)")
            P.add("sp", lambda e: e.dma_start(out=dtb[:], in_=bass.AP(gdtb_d.tensor, 0, [[0, 128], [0, 16], [1, 16]])),
                  writes=["g_dtb"], dma=True)
            P.add("sp", lambda e: e.dma_start(out=eA[:], in_=bass.AP(galog_d.tensor, 0, [[0, 128], [0, 16], [1, 16]])),
                  writes=["g_eA"], dma=True)
            wbg = sb("wbg", [128, 8, 32], BF16)
            P.add("pool", lambda e: e.dma_start(out=wbg[:], in_=gsrc[:, :, 6144:6176]), writes=["wbg"], dma=True)
            for c in range(16):
                for kc in range(8):
                    MM(psum[:, 0, c * 32:(c + 1) * 32], hT[:, kc, c * 128:(c + 1) * 128], wbg[:, kc, :],
                       [("hT", c), "wbg"], PS(0), start=(kc == 0), stop=(kc == 7))
            pbg = psum[:, 0, :].rearrange("p (c n) -> p c n", n=32)
            ACTF(beta[:], pbg[:, :, 0:16], AF.Sigmoid, PS(0), ["g_beta"])
            TT("dve", tmp1[:], pbg[:, :, 16:32], dtb[:], ALU.add, PS(0) + ["g_dtb"], ["g_tmp1"])
            ACTF(tmp1[:], tmp1[:], AF.Exp, ["g_tmp1"], ["g_tmp1"])
            ACTF(tmp1[:], tmp1[:], AF.Ln, ["g_tmp1"], ["g_tmp1"], bias=1.0)
            ACTF(eA[:], eA[:], AF.Exp, ["g_eA"], ["g_eA"])
            STT(graw[:], tmp1[:], -1.0, eA[:], ALU.mult, ALU.mult, ["g_tmp1", "g_eA"], ["g_graw"])
            MM(psum[:, 1, 0:256], Tri[:], fl(graw), ["Tri", "g_graw"], PS(1))
            MM(psum[:, 2, 0:256], onesF[:], fl(graw), ["onesF", "g_graw"], PS(2))
            ACTF(fl(cg), psum[:, 1, 0:256], AF.Copy, PS(1), ["g_cg"])
            ACTF(fl(gl), psum[:, 2, 0:256], AF.Copy, PS(2), ["g_gl"])
            ACTF(fl(eg), psum[:, 1, 0:256], AF.Exp, PS(1), ["g_eg"])
            ACTF(fl(sdec), psum[:, 2, 0:256], AF.Exp, PS(2), ["g_sdec"])
            TT("dve", fl(gl), fl(gl), fl(cg), ALU.subtract, ["g_gl", "g_cg"], ["g_gl"])
            ACTF(fl(kd), fl(gl), AF.Exp, ["g_gl"], ["g_kd"])
            TT("dve", fl(bgam), fl(beta), fl(eg), ALU.mult, ["g_beta", "g_eg"], ["g_bgam"])

            wrot = Rot("g_w", [sb(f"g_w{i}", [128, 8, 128], BF16) for i in range(3)])
            wz = [sb(f"g_wz{i}", [128, 8, 128], BF16) for i in range(2)]
            craw = sb("g_craw", [128, 3 + T])
            P.add("pool", lambda e: e.memset(craw[:, 0:3], 0.0), writes=["craw_pad"])
            xh = lambda k, i: xbufs[k][:, i * 512:(i + 1) * 512]
            t1 = Rot("xsA0", [xh(0, 0)])
            s1 = Rot("xsA0b", [xh(0, 1)])
            s1.next = lambda: (xh(0, 1), ("xsA0", 1))
            o_za = (xh(1, 0), ("xsA1", 0))
            o_oa = (xh(1, 1), ("xsA1", 1))
            o_qa = (wn[:, 0:512], "wn")
            o_ra = (wn[:, 512:1024], "wn")
            sq = Rot("g_sq", [sb(f"g_sq{i}", [128, 512]) for i in range(1)])
            rn = Rot("g_rn", [sb(f"g_rn{i}", [128, 512]) for i in range(1)])
            qT = sb("g_qT", [128, T], BF16)
            kT = sb("g_kT", [128, T], BF16)
            vT = [sb(f"g_vT{i}", [128, T], BF16) for i in range(2)]
            pj_bank = Rot("pjb", [0, 1])
            NI = 4
            b4 = lambda n: sb(n, [128, NI, 128], BF16)
            f4 = lambda n: sb(n, [128, NI, 128])
            ndL = f4("g_ndL")
            ndU = f4("g_ndU")
            grow = f4("g_grow")
            Abuf = b4("g_A")
            Msb = b4("g_Msb")
            kbg = b4("g_kbg")
            vb = b4("g_vb")
            qkmT = Rot("g_qkmT", [b4(f"g_qkmT{i}") for i in range(2)])
            qdT = Rot("g_qdT", [b4(f"g_qdT{i}") for i in range(2)])
            kdec = Rot("g_kdec", [b4(f"g_kdec{i}") for i in range(2)])
            u_sb = Rot("g_u", [f4(f"g_u{i}") for i in range(2)])
            wT_sb = Rot("g_wT", [b4(f"g_wT{i}") for i in range(2)])
            vnew = Rot("g_vnew", [sb(f"g_vnew{i}", [128, 2, 128], BF16) for i in range(2)])
            S = sb("g_S", [128, 2, 128])
            Sbf = sb("g_Sbf", [128, 2, 128], BF16)
            pT1 = pbank(1).bitcast(BF16)
            QSCALE = 128.0 ** -0.5
            print("gdn sbuf bytes remaining", nc.sbuf_bytes_remaining)

            def proj_tile(col0, kind, dest, dres):
                j = col0 // 128
                wa, wr = wrot.next()
                P.add("pool", lambda e: e.dma_start(out=wa[:], in_=gsrc[:, :, col0:col0 + 128]), writes=[wr], dma=True)
                for nb in range(4):
                    c0 = nb * 512
                    hres = [("hT", 4 * nb + i) for i in range(4)]
                    bk, _ = pj_bank.next()
                    for kc in range(8):
                        MM(pbank(bk), wa[:, kc, :], hT[:, kc, c0:c0 + 512], [wr] + hres, PS(bk),
                           start=(kc == 0), stop=(kc == 7))
                    ACTF(craw[:, 3 + c0:3 + c0 + 512], pbank(bk), AF.Copy, PS(bk), [("craw", nb)])
                    yield
                    halo = [("craw", nb - 1)] if nb > 0 else ["craw_pad"]
                    ta, tr = t1.next()
                    TS("pool", ta, craw[:, c0:c0 + 512], gcw[:, 0, j:j + 1], 0.0, ALU.mult, ALU.add,
                       [("craw", nb)] + halo + gcwr, [tr])
                    for k in (1, 2, 3):
                        STT(ta, craw[:, c0 + k:c0 + k + 512], gcw[:, k, j:j + 1], ta, ALU.mult, ALU.add,
                            [("craw", nb), tr] + halo + gcwr, [tr])
                    yield
                    if kind == "v":
                        ACTF(dest[:, c0:c0 + 512], ta, AF.Silu, [tr], [dres + (nb,)])
                        continue
                    sa, sr = s1.next()
                    ACTF(sa, ta, AF.Silu, [tr], [sr])
                    qa, qr = sq.next()
                    ACTF(qa[:], sa, AF.Square, [sr], [qr])
                    bk2, _ = pj_bank.next()
                    yield
                    MM(pbank(bk2), onesF[:], qa[:], ["onesF", qr], PS(bk2))
                    ra, rr = rn.next()
                    ACTF(ra[:], pbank(bk2), AF.Ln, PS(bk2), [rr], bias=EPS)
                    ACTF(ra[:], ra[:], AF.Exp, [rr], [rr], scale=-0.5)
                    yield
                    STT(dest[:, c0:c0 + 512], sa, (QSCALE if kind == "q" else 1.0), ra[:], ALU.mult, ALU.mult,
                        [sr, rr], [dres + (nb,)])

            def bufset(p):
                if p % 2 == 1 and p < 6:
                    aps = [yT[:, 12 + k, :] for k in range(4)]
                    res = [("yT", 12 + k) for k in range(4)]
                else:
                    aps = [qT[:], kT[:], vT[0][:], vT[1][:]]
                    res = [("g_qT",), ("g_kT",), ("g_vT0",), ("g_vT1",)]
                return aps, res

            def proj_gen(p):
                aps, res = bufset(p)
                yield from proj_tile(p * 128, "q", aps[0], res[0])
                yield from proj_tile(1024 + p * 128, "k", aps[1], res[1])
                for hh in range(2):
                    yield from proj_tile(2048 + (2 * p + hh) * 128, "v", aps[2 + hh], res[2 + hh])

            def drive(gens):
                gens = list(gens)
                while gens:
                    for g in list(gens):
                        try:
                            next(g)
                        except StopIteration:
                            gens.remove(g)

            v3 = lambda ap: ap.rearrange("p (a b) -> p a b", b=128)
            live01 = [0]

            def drive_bg(gens, bg, ratio=5):
                gens = list(gens)
                n = 0
                while gens:
                    for g in list(gens):
                        try:
                            next(g)
                        except StopIteration:
                            gens.remove(g)
                        n += 1
                        if bg[0] is not None and n % ratio == 0 and live01[0] == 0:
                            try:
                                next(bg[0])
                            except StopIteration:
                                bg[0] = None

            overlap_ok = lambda pn: pn < DBG_NPAIRS and bufset(pn)[1] != bufset(pn - 1)[1]
            drive([proj_gen(0)])
            for p in range(DBG_NPAIRS):
                hs = (2 * p, 2 * p + 1)
                (qA, kA, v0A, v1A), (qN, kN, v0N, v1N) = bufset(p)
                vA, vN = (v0A, v1A), (v0N, v1N)
                for hh in range(2):
                    zsrc = gsrc[:, :, 4096 + hs[hh] * 128:4096 + (hs[hh] + 1) * 128]
                    P.add("pool", lambda e, hh=hh, zsrc=zsrc: e.dma_start(out=wz[hh][:], in_=zsrc),
                          writes=[("g_wz", hh)], dma=True)
                bg = [proj_gen(p + 1) if overlap_ok(p + 1) else None]
                P.add("pool", lambda e: e.memset(S[:], 0.0), writes=[("g_S", 0), ("g_S", 1)])
                P.add("pool", lambda e: e.memset(Sbf[:], 0.0), writes=["g_Sbf"])
                st_ = {}
                if DBG_STAGE < 2:
                    continue

                def prep2(cp):
                    insts = [(cb, hh) for cb in range(2) for hh in range(2)]
                    live01[0] += 1
                    for cb in range(2):
                        c = 2 * cp + cb
                        cs = slice(c * 128, (c + 1) * 128)
                        nb = c // 4
                        qres, kres = qN + (nb,), kN + (nb,)
                        MM(psum[:, 0, cb * 256:cb * 256 + 128], kA[:, cs], kA[:, cs], [kres], PS(0))
                        MM(psum[:, 0, cb * 256 + 128:cb * 256 + 256], kA[:, cs], qA[:, cs], [kres, qres], PS(0))
                        TR(pT1[:, cb * 128:(cb + 1) * 128], kA[:, cs], [kres], PS(1))
                        for hh in range(2):
                            i = cb * 2 + hh
                            TR(pT1[:, 256 + i * 128:256 + (i + 1) * 128], vA[hh][:, cs], [vN[hh] + (nb,)], PS(1))
                    for i, (cb, hh) in enumerate(insts):
                        c = 2 * cp + cb
                        TS("pool", GR[:, i, :], onesF[:], graw[:, c, hs[hh]:hs[hh] + 1], 0.0, ALU.mult, ALU.add,
                           ["onesF", "g_graw"], [("g_GR", i)])
                        MM(psum[:, 2, i * 128:(i + 1) * 128], GR[:, i, :], Tri[:], [("g_GR", i), "Tri"], PS(2))
                    yield
                    for i, (cb, hh) in enumerate(insts):
                        c, h = 2 * cp + cb, hs[hh]
                        gb = psum[:, 2, i * 128:(i + 1) * 128]
                        STT(ndL[:, i, :], gb, cg[:, c, h:h + 1], maskL[:], ALU.subtract, ALU.max,
                            PS(2) + ["g_cg", "maskL"], [("g_ndL", i)])
                        STT(ndU[:, i, :], gb, cg[:, c, h:h + 1], maskU[:], ALU.subtract, ALU.min,
                            PS(2) + ["g_cg", "maskU"], [("g_ndU", i)])
                    allr = lambda n: [(n, i) for i in range(NI)]
                    ACTF(grow[:], v3(pbank(2)), AF.Exp, PS(2), allr("g_grow"))
                    ACTF(ndL[:], ndL[:], AF.Exp, allr("g_ndL"), allr("g_ndL"), scale=-1.0)
                    ACTF(ndU[:], ndU[:], AF.Exp, allr("g_ndU"), allr("g_ndU"))
                    yield
                    qk_a, qk_r = qkmT.next()
                    qd_a, qd_r = qdT.next()
                    kd_a, kd_r = kdec.next()
                    for i, (cb, hh) in enumerate(insts):
                        c, h = 2 * cp + cb, hs[hh]
                        cs = slice(c * 128, (c + 1) * 128)
                        STT(Abuf[:, i, :], psum[:, 0, cb * 256:cb * 256 + 128], beta[:, c, h:h + 1], ndL[:, i, :],
                            ALU.mult, ALU.mult, PS(0) + ["g_beta", ("g_ndL", i)], [("g_A", i)])
                        TT("dve", qk_a[:, i, :], psum[:, 0, cb * 256 + 128:cb * 256 + 256], ndU[:, i, :], ALU.mult,
                           PS(0) + [("g_ndU", i)], [(qk_r, i)])
                        TT("pool", qd_a[:, i, :], qA[:, cs], grow[:, i, :], ALU.mult,
                           [qN + (c // 4,), ("g_grow", i)], [(qd_r, i)])
                        AMUL(kbg[:, i, :], pT1[:, cb * 128:(cb + 1) * 128], bgam[:, c, h:h + 1],
                             PS(1) + ["g_bgam"], [("g_kbg", i)])
                        AMUL(kd_a[:, i, :], pT1[:, cb * 128:(cb + 1) * 128], kd[:, c, h:h + 1],
                             PS(1) + ["g_kd"], [(kd_r, i)])
                        AMUL(vb[:, i, :], pT1[:, 256 + i * 128:256 + (i + 1) * 128], beta[:, c, h:h + 1],
                             PS(1) + ["g_beta"], [("g_vb", i)])
                    live01[0] -= 1
                    yield
                    cur, curr = UT0, [["UT0"], ["UT0"]]
                    p4 = [psum[:, 4, :].rearrange("p (a b c) -> p a b c", a=2, b=2),
                          psum[:, 7, :].rearrange("p (a b c) -> p a b c", a=2, b=2)]
                    mbank = (3, 2)
                    pbk = (4, 7)
                    for lv in range(7):
                        for half in range(2):
                            for k in range(2):
                                i = 2 * half + k
                                MM(psum[:, mbank[half], k * 128:(k + 1) * 128], Abuf[:, i, :], cur[:, i, 0, :],
                                   [("g_A", i)] + curr[half], PS(mbank[half]))
                        yield
                        for half in range(2):
                            TT("dve", Msb[:, 2 * half:2 * half + 2, :], v3(psum[:, mbank[half], 0:256]),
                               maskB[:, lv, 2 * half:2 * half + 2, :], ALU.mult,
                               PS(mbank[half]) + [("maskB", lv)], [("g_Msb", half)])
                            for k in range(2):
                                i = 2 * half + k
                                pp = p4[half]
                                MM(pp[:, k, 0, :], cur[:, i, 1, :], Msb[:, i, :], curr[half] + [("g_Msb", half)],
                                   PS(pbk[half]))
                                if lv < 6:
                                    MM(pp[:, k, 1, :], Msb[:, i, :], cur[:, i, 1, :], curr[half] + [("g_Msb", half)],
                                       PS(pbk[half]))
                        yield
                        nxt = UTw[lv % 2]
                        nxtr = [[("UTw", lv % 2, 0)], [("UTw", lv % 2, 1)]]
                        for half in range(2):
                            STT(nxt[:, 2 * half:2 * half + 2].rearrange("p a b c -> p (a b c)"),
                                pbank(pbk[half]), -1.0,
                                cur[:, 2 * half:2 * half + 2].rearrange("p a b c -> p (a b c)"),
                                ALU.mult, ALU.add, curr[half] + PS(pbk[half]), nxtr[half])
                        cur, curr = nxt, nxtr
                    curr = curr[0] + curr[1]
                    for i in range(NI):
                        MM(psum[:, 2, i * 128:(i + 1) * 128], cur[:, i, 0, :], vb[:, i, :], curr + [("g_vb", i)], PS(2))
                    for i in range(NI):
                        MM(psum[:, 3, i * 128:(i + 1) * 128], kbg[:, i, :], cur[:, i, 0, :], curr + [("g_kbg", i)], PS(3))
                    yield
                    u_a, u_r = u_sb.next()
                    w_a, w_r = wT_sb.next()
                    ACTF(u_a[:], v3(pbank(2)), AF.Copy, PS(2), [u_r])
                    ACTF(w_a[:], v3(pbank(3)), AF.Copy, PS(3), [w_r])
                    st_[cp] = dict(u=(u_a, u_r), w=(w_a, w_r), qk=(qk_a, qk_r), qd=(qd_a, qd_r), kd=(kd_a, kd_r))

                def recur(c):
                    d = st_[c // 2]
                    (u_a, u_r), (w_a, w_r), (qk_a, qk_r), (qd_a, qd_r), (kd_a, kd_r) = d["u"], d["w"], d["qk"], d["qd"], d["kd"]
                    cb = c % 2
                    i0_ = cb * 2
                    for hh in range(2):
                        MM(psum[:, 5, hh * 128:(hh + 1) * 128], w_a[:, i0_ + hh, :], Sbf[:, hh, :], [w_r, "g_Sbf"], PS(5))
                    yield
                    vn_a, vn_r = vnew.next()
                    STT(vn_a[:].rearrange("p a b -> p (a b)"), psum[:, 5, 0:256], -1.0,
                        u_a[:, i0_:i0_ + 2, :].rearrange("p a b -> p (a b)"), ALU.mult, ALU.add, [u_r] + PS(5), [vn_r])
                    for hh in range(2):
                        ocol = hh * 256 + cb * 128
                        MM(psum[:, 6, ocol:ocol + 128], Sbf[:, hh, :], qd_a[:, i0_ + hh, :], ["g_Sbf", (qd_r, i0_ + hh)],
                           PS(6), start=True, stop=False)
                        MM(psum[:, 6, ocol:ocol + 128], vn_a[:, hh, :], qk_a[:, i0_ + hh, :], [vn_r, (qk_r, i0_ + hh)],
                           PS(6), start=False, stop=True)
                        MM(psum[:, 5, 256 + hh * 128:256 + (hh + 1) * 128], kd_a[:, i0_ + hh, :], vn_a[:, hh, :],
                           [(kd_r, i0_ + hh), vn_r], PS(5))
                    yield
                    for hh in range(2):
                        STT(S[:, hh, :], S[:, hh, :], sdec[:, c, hs[hh]:hs[hh] + 1],
                            psum[:, 5, 256 + hh * 128:256 + (hh + 1) * 128], ALU.mult, ALU.add,
                            [("g_S", hh), "g_sdec"] + PS(5), [("g_S", hh)])
                    ACTF(Sbf[:], S[:], AF.Copy, [("g_S", 0), ("g_S", 1)], ["g_Sbf"])
                    if cb == 1:
                        t0_ = (c - 1) * 128
                        hres = [("hT", c - 1), ("hT", c)]
                        yield
                        live01[0] += 1
                        for hh in range(2):
                            for kc in range(8):
                                MM(psum[:, 0, hh * 256:(hh + 1) * 256], wz[hh][:, kc, :], hT[:, kc, t0_:t0_ + 256],
                                   [("g_wz", hh)] + hres, PS(0), start=(kc == 0), stop=(kc == 7))
                        qa, qr = o_qa
                        ACTF(qa, pbank(6), AF.Square, PS(6), [qr])
                        yield
                        MM(pbank(1), onesF[:], qa, ["onesF", qr], PS(1))
                        za, zr = o_za
                        ACTF(za, pbank(0), AF.Silu, PS(0), [zr])
                        yield
                        ra, rr = o_ra
                        ACTF(ra, pbank(1), AF.Ln, PS(1), [rr], scale=1.0 / 128, bias=EPS)
                        live01[0] -= 1
                        yield
                        ACTF(ra, ra, AF.Exp, [rr], [rr], scale=-0.5)
                        yield
                        oa, orr = o_oa
                        STT(oa, pbank(6), gnw[:, 0:1], ra, ALU.mult, ALU.mult, PS(6) + ["gnw", rr], [orr])
                        yield
                        TT("pool", yT[:, hs[0]:hs[0] + 2, t0_:t0_ + 256], oa.rearrange("p (a b) -> p a b", a=2),
                           za.rearrange("p (a b) -> p a b", a=2), ALU.mult, [orr, zr],
                           [("yT", hs[0], c // 4), ("yT", hs[1], c // 4)])

                def tail(cp):
                    yield from recur(2 * cp)
                    yield
                    yield from recur(2 * cp + 1)

                ncp = DBG_NCHUNK // 2
                drive_bg([prep2(0)], bg)
                for cp in range(ncp):
                    gens = [tail(cp)]
                    if cp + 1 < ncp:
                        gens.insert(0, prep2(cp + 1))
                    drive_bg(gens, bg)
                if bg[0] is not None:
                    drive([bg[0]])
                elif p + 1 < DBG_NPAIRS and not overlap_ok(p + 1):
                    drive([proj_gen(p + 1)])

        def load_xs(xs):
            for t in range(NT):
                P.add("sp", lambda e, t=t: e.dma_start(out=xs[:, t, :], in_=x_d[t * 128:(t + 1) * 128, :]),
                      writes=[("xs", t)], dma=True)

        xs = None
        scr = None
        for li, l in enumerate(layers):
            final = li == len(layers) - 1
            if l == 0:
                with ExitStack() as s0:
                    sb0 = mk_sb(s0)
                    xbufs = [sb0(f"xstream{i}", [128, D]) for i in range(2)]

                    def xsrc(t):
                        k = t % 2
                        P.add("sp", lambda e, t=t, k=k: e.dma_start(out=xbufs[k][:], in_=x_d[t * 128:(t + 1) * 128, :]),
                              writes=[(f"xsA{k}", 0), (f"xsA{k}", 1)], dma=True)
                        return xbufs[k][:], [(f"xsA{k}", 0), (f"xsA{k}", 1)]
                    phase_norm(0, xsrc)
                    phase_gdn(sb0, xbufs)
                P.barrier()
                xs = sb("xs", [128, NT, D])
                scr = (sb("scrA", [128, D]), sb("scrB", [128, D]))
                load_xs(xs)
                phase_out(0, gwout_d, final, xs, sb, *scr)
            else:
                if xs is None:
                    xs = sb("xs", [128, NT, D])
                    scr = (sb("scrA", [128, D]), sb("scrB", [128, D]))
                    load_xs(xs)
                phase_norm(1, lambda t: (xs[:, t, :], [("xs", t)]), stats_first=True)
                phase_sc(sb, *scr)
                phase_out(1, swout_d, final, xs, sb, *scr)
        P.emit(st)
        global LAST_P
        LAST_P = P
    return nc


_NC_CACHE = {}


def _get_nc(layers):
    if layers not in _NC_CACHE:
        _NC_CACHE[layers] = build_program(layers)
    return _NC_CACHE[layers]


def _in_maps(inputs, xin):
    f = lambda a: np.ascontiguousarray(np.asarray(a, dtype=np.float32))
    shared = {
        "pre_norm_w": f(inputs["pre_norm_w"]),
        "post_norm_w": f(inputs["post_norm_w"]),
        "gdn_w_in": f(inputs["gdn_w_in"][0]),
        "gdn_conv_w": f(inputs["gdn_conv_w"][0]),
        "gdn_A_log": f(inputs["gdn_A_log"]).reshape(1, 16),
        "gdn_dt_bias": f(inputs["gdn_dt_bias"]).reshape(1, 16),
        "gdn_norm_w": f(inputs["gdn_norm_w"]).reshape(128, 1),
        "gdn_w_out": f(inputs["gdn_w_out"][0]),
        "sc_w_in": f(inputs["sc_w_in"][0]),
        "sc_conv_w": f(inputs["sc_conv_w"][0]),
        "sc_w_out": f(inputs["sc_w_out"][0]),
    }
    return [dict(shared, x=f(xin[b])) for b in range(8)]


def run_layers(inputs, layers, xin=None):
    x = np.asarray(inputs["x"], dtype=np.float32) if xin is None else xin
    nc = _get_nc(tuple(layers))
    res = run_bass_kernel_spmd(nc, _in_maps(inputs, x), core_ids=list(range(8)))
    return np.stack([np.asarray(r["out"], dtype=np.float32) for r in res.results], axis=0)


FUSED = True


def kernel(**inputs):
    if FUSED:
        return run_layers(inputs, (0, 1))
    x1 = run_layers(inputs, (0,))
    return run_layers(inputs, (1,), xin=x1)
```

```python
from contextlib import ExitStack

import numpy as np
import concourse.bass as bass
import concourse.mybir as mybir
from concourse.bass_utils import run_bass_kernel_spmd

F32 = mybir.dt.float32
BF16 = mybir.dt.bfloat16
ALU = mybir.AluOpType
AF = mybir.ActivationFunctionType

T = 2048
D = 1024
NT = T // 128
EPS = 1e-6
GDN_PROJ = 6176

ENGS = ("pe", "act", "dve", "pool", "sp")
DBG_NPAIRS = 8
DBG_NCHUNK = 16
DBG_STAGE = 9
DBG_PREP_STEPS = 0
DBG_PREP2_STEPS = 0
DBG_ROT1 = 0
DBG_SKIP = 0
N_DMA_SEMS = 40


class Op:
    __slots__ = ("eng", "fn", "deps", "signal", "cnt", "is_dma", "dslot", "dval", "is_out")


class Prog:
    def __init__(self, nc):
        self.nc = nc
        self.ops = {e: [] for e in ENGS}
        self.res = {}
        self.dma_rr = {"sp": 0, "pool": 0, "act": 0}
        self.dma_last = [None] * N_DMA_SEMS
        self.dma_val = [0] * N_DMA_SEMS
        self.out_dmas = []
        self.bar_deps = []
        self.bar_pending = set()

    def barrier(self):
        deps = [self.ops[e][-1] for e in ENGS if self.ops[e] and not self.ops[e][-1].is_dma]
        for e in ENGS:
            for o in reversed(self.ops[e]):
                if not o.is_dma:
                    if o not in deps:
                        deps.append(o)
                    break
        deps += [d for d in self.dma_last if d is not None]
        self.bar_deps = deps
        self.bar_pending = set(ENGS)

    def add(self, eng, fn, reads=(), writes=(), dma=False, is_out=False):
        op = Op()
        op.eng, op.fn, op.is_dma, op.signal, op.cnt, op.is_out = eng, fn, dma, False, 0, is_out
        deps = []
        if eng in self.bar_pending:
            self.bar_pending.discard(eng)
            deps.extend(d for d in self.bar_deps if d.is_dma or d.eng != eng or dma)

        def need(d, raw):
            if d is op:
                return
            if d.is_dma:
                deps.append(d)
            elif d.eng == eng and not dma:
                if raw and eng != "pe":
                    deps.append(d)
            else:
                deps.append(d)

        excl = [r for r in reads if isinstance(r, tuple) and r[0] == "ps"]
        if excl:
            reads = [r for r in reads if r not in excl]
            for r in excl:
                st = self.res.get(r)
                if st:
                    for d in st[0]:
                        need(d, True)
            writes = list(writes) + excl
        for r in reads:
            st = self.res.get(r)
            if st:
                for d in st[0]:
                    need(d, True)
        for r in writes:
            st = self.res.get(r)
            if st:
                for d in st[0]:
                    need(d, False)
                for d in st[1].values():
                    need(d, False)
                for d in st[2]:
                    need(d, False)
        for r in reads:
            st = self.res.setdefault(r, [[], {}, []])
            if dma:
                st[2].append(op)
            else:
                st[1][eng] = op
        for r in writes:
            self.res[r] = [[op], {}, []]
        if dma:
            half = N_DMA_SEMS // 2
            base = 0 if eng == "sp" else half
            slot = base + self.dma_rr[eng]
            self.dma_rr[eng] = (self.dma_rr[eng] + 1) % half
            prev = self.dma_last[slot]
            if prev is not None:
                deps.append(prev)
            self.dma_val[slot] += 16
            op.dslot, op.dval = slot, self.dma_val[slot]
            self.dma_last[slot] = op
            if is_out:
                self.out_dmas.append(op)
        op.deps = deps
        for d in deps:
            if not d.is_dma:
                d.signal = True
        self.ops[eng].append(op)
        return op

    def emit(self, stack):
        nc = self.nc
        esem = {e: stack.enter_context(nc.semaphore(f"s_{e}")) for e in ENGS}
        dsem = [stack.enter_context(nc.semaphore(f"s_d{i}")) for i in range(N_DMA_SEMS)]
        for e in ENGS:
            c = 0
            for op in self.ops[e]:
                if op.signal and not op.is_dma:
                    c += 1
                    op.cnt = c
        block = stack.enter_context(nc.Block())
        outs = self.out_dmas

        def run(e, engine):
            waited = {}

            def w(sem, key, val):
                if waited.get(key, 0) >= val:
                    return
                waited[key] = val
                engine.wait_ge(sem, val)

            for op in self.ops[e]:
                for d in op.deps:
                    if d.is_dma:
                        w(dsem[d.dslot], ("d", d.dslot), d.dval)
                    else:
                        w(esem[d.eng], d.eng, d.cnt)
                ins = op.fn(engine)
                if op.is_dma:
                    ins.then_inc(dsem[op.dslot], 16)
                elif op.signal:
                    ins.then_inc(esem[e], 1)
            if e == "sp":
                for d in outs:
                    w(dsem[d.dslot], ("d", d.dslot), d.dval)

        @block.tensor
        def _(engine):
            run("pe", engine)

        @block.scalar
        def _(engine):
            run("act", engine)

        @block.vector
        def _(engine):
            run("dve", engine)

        @block.gpsimd
        def _(engine):
            run("pool", engine)

        @block.sync
        def _(engine):
            run("sp", engine)


class Rot:
    def __init__(self, name, aps):
        self.name, self.aps, self.i = name, aps, 0

    def next(self):
        k = self.i % (1 if DBG_ROT1 else len(self.aps))
        self.i += 1
        return self.aps[k], (self.name, k)


def build_program(layers=(0, 1)):
    nc = bass.Bass("TRN2", target_bir_lowering=False)
    dt = lambda n, s, kind="ExternalInput": nc.dram_tensor(n, s, F32, kind=kind).ap()
    x_d = dt("x", [T, D])
    out_d = dt("out", [T, D], "ExternalOutput")
    pre_d = dt("pre_norm_w", [2, D])
    post_d = dt("post_norm_w", [2, D])
    gwin_d = dt("gdn_w_in", [D, GDN_PROJ])
    gcw_d = dt("gdn_conv_w", [4, 4096])
    galog_d = dt("gdn_A_log", [1, 16])
    gdtb_d = dt("gdn_dt_bias", [1, 16])
    gnw_d = dt("gdn_norm_w", [128, 1])
    gwout_d = dt("gdn_w_out", [2048, D])
    swin_d = dt("sc_w_in", [D, 8192])
    scw_d = dt("sc_conv_w", [3, 2048])
    swout_d = dt("sc_w_out", [2048, D])

    with ExitStack() as st:
        def mk_sb(stack):
            def sb(name, shape, dty=F32):
                return stack.enter_context(nc.sbuf_tensor(name, shape, dty))
            return sb
        sb = mk_sb(st)

        P = Prog(nc)
        hT = sb("hT", [128, 8, T], BF16)
        wout = hT[:].rearrange("p a b -> p (a b)").rearrange("p (j c) -> p j c", c=D)
        yT = sb("yT", [128, 16, T], BF16)
        ident = sb("ident", [128, 128], BF16)
        wn = sb("wn", [128, D])
        junk = sb("junk", [128, D], BF16)
        ss = sb("ss", [128, NT])
        sd = sb("sd", [128, NT])
        rstd = sb("rstd", [128, NT])
        hn = [sb(f"hn{i}", [128, D], BF16) for i in range(2)]
        psum = st.enter_context(nc.psum_tensor("psum", [128, 8, 512], F32))
        pbank = lambda b: psum[:, b, :]
        hT_all = [("hT", t) for t in range(NT)]
        BANKRES = {b: [("ps", b)] for b in range(8)}
        PS = lambda b: BANKRES[b]

        def MM(out, lhsT, rhs, reads, writes, start=True, stop=True):
            return P.add("pe", lambda e: e.matmul(out, lhsT=lhsT, rhs=rhs, start=start, stop=stop), reads, writes)

        def TR(out, in_, reads, writes):
            return P.add("pe", lambda e: e.transpose(out=out, in_=in_, identity=ident[:]), list(reads) + ["ident"], writes)

        def ACTF(out, in_, func, reads, writes, **kw):
            return P.add("act", lambda e: e.activation(out=out, in_=in_, func=func, **kw), reads, writes)

        def AMUL(out, in_, mul, reads, writes):
            return P.add("act", lambda e: e.mul(out, in_, mul), reads, writes)

        def TT(eng, out, in0, in1, op, reads, writes):
            return P.add(eng, lambda e: e.tensor_tensor(out=out, in0=in0, in1=in1, op=op), reads, writes)

        def STT(out, in0, scalar, in1, op0, op1, reads, writes):
            return P.add("dve", lambda e: e.scalar_tensor_tensor(out=out, in0=in0, scalar=scalar, in1=in1,
                                                                 op0=op0, op1=op1), reads, writes)

        def TS(eng, out, in0, s1, s2, op0, op1, reads, writes):
            return P.add(eng, lambda e: e.tensor_scalar(out=out, in0=in0, scalar1=s1, scalar2=s2, op0=op0, op1=op1),
                         reads, writes)

        P.add("pool", lambda e: e.memset(ident[:], 1.0), writes=["ident"])
        P.add("pool", lambda e: e.affine_select(out=ident[:], in_=ident[:], pattern=[[-1, 128]],
                                                compare_op=ALU.is_equal, fill=0.0, base=0,
                                                channel_multiplier=1), reads=["ident"], writes=["ident"])

        def phase_norm(l, xsrc, stats_first=False):
            P.add("sp", lambda e: e.dma_start(out=wn[:], in_=pre_d[l].partition_broadcast(128)),
                  writes=["wn"], dma=True)

            def stats(t, xa, xr):
                P.add("act", lambda e, t=t, xa=xa: e.activation(out=junk[:], in_=xa, func=AF.Square,
                                                                accum_out=ss[:, t:t + 1]),
                      reads=xr, writes=["junk", ("ss", t)])
                P.add("act", lambda e, t=t: e.activation(out=sd[:, t:t + 1], in_=ss[:, t:t + 1], func=AF.Sqrt,
                                                         scale=1.0 / D, bias=EPS),
                      reads=[("ss", t)], writes=[("sd", t)])
                P.add("dve", lambda e, t=t: e.reciprocal(out=rstd[:, t:t + 1], in_=sd[:, t:t + 1]),
                      reads=[("sd", t)], writes=[("rstd", t)])

            srcs = {}
            if stats_first:
                for t in range(NT):
                    srcs[t] = xsrc(t)
                    stats(t, *srcs[t])
            for t in range(NT):
                if t not in srcs:
                    srcs[t] = xsrc(t)
                    stats(t, *srcs[t])
                xa, xr = srcs[t]
                h = hn[t % 2]
                hr = ("hn", t % 2)
                P.add("dve", lambda e, t=t, h=h, xa=xa: e.scalar_tensor_tensor(
                    out=h[:], in0=xa, scalar=rstd[:, t:t + 1], in1=wn[:],
                    op0=ALU.mult, op1=ALU.mult), reads=xr + [("rstd", t), "wn"], writes=[hr])
                b = t % 2
                pT = pbank(b).bitcast(BF16)
                for dc in range(8):
                    P.add("pe", lambda e, h=h, dc=dc, pT=pT: e.transpose(
                        out=pT[:, dc * 128:(dc + 1) * 128], in_=h[:, dc * 128:(dc + 1) * 128], identity=ident[:]),
                        reads=[hr, "ident"], writes=PS(b))
                P.add("act", lambda e, t=t, pT=pT: e.copy(
                    out=hT[:, :, t * 128:(t + 1) * 128], in_=pT.rearrange("p (a b) -> p a b", b=128)),
                    reads=PS(b), writes=[("hT", t)])

        def phase_sc(sb, scrA, scrB):
            cw = sb("sc_cw", [128, 3, 16])
            for k in range(3):
                P.add("sp", lambda e, k=k: e.dma_start(out=cw[:, k, :], in_=scw_d[k].rearrange("(j p) -> p j", p=128),
                                                       allow_slow_non_contiguous=True),
                      writes=[("sc_cw", k)], dma=True)
            wrot = Rot("sc_w", [sb(f"sc_w{i}", [128, 8, 128], BF16) for i in range(5)])
            cu = sb("sc_cu", [128, 2 + T])
            u_sb = Rot("scrA", [scrA[:, 0:512], scrA[:, 512:1024]])
            sz = Rot("scrB", [scrB[:, 0:512], scrB[:, 512:1024]])
            t1 = Rot("sc_t1", [sb(f"sc_t1{i}", [128, 512]) for i in range(2)])
            gg = Rot("sc_g", [sb(f"sc_g{i}", [128, 512]) for i in range(2)])
            P.add("pool", lambda e: e.memset(cu[:, 0:2], 0.0), writes=["cu_pad"])
            src = swin_d.rearrange("(kc p) (ty j c) -> p kc ty j c", p=128, ty=4, c=128)
            for j in range(16):
                wts = []
                for ty in range(4):
                    wa, wr = wrot.next()
                    P.add("pool", lambda e, wa=wa, j=j, ty=ty: e.dma_start(out=wa[:], in_=src[:, :, ty, j, :]),
                          writes=[wr], dma=True)
                    wts.append((wa, wr))
                for nb in range(4):
                    c0 = nb * 512
                    hres = [("hT", 4 * nb + i) for i in range(4)]
                    bk = [(nb % 2) * 4 + ty for ty in range(4)]
                    for ty in range(4):
                        wa, wr = wts[ty]
                        for kc in range(8):
                            P.add("pe", lambda e, wa=wa, kc=kc, c0=c0, b=bk[ty]: e.matmul(
                                pbank(b), lhsT=wa[:, kc, :], rhs=hT[:, kc, c0:c0 + 512],
                                start=(kc == 0), stop=(kc == 7)),
                                reads=[wr] + hres, writes=PS(bk[ty]))
                    pu, pB, pC, pz = [pbank(b) for b in bk]
                    ru, rB, rC, rz = [PS(b) for b in bk]
                    ua, ur = u_sb.next()
                    P.add("act", lambda e, ua=ua, pu=pu: e.copy(out=ua, in_=pu), reads=ru, writes=[ur])
                    P.add("dve", lambda e, ua=ua, pC=pC, c0=c0: e.tensor_tensor(
                        out=cu[:, 2 + c0:2 + c0 + 512], in0=pC, in1=ua, op=ALU.mult),
                        reads=rC + [ur], writes=[("cu", nb)])
                    za, zr = sz.next()
                    P.add("act", lambda e, za=za, pz=pz: e.activation(out=za, in_=pz, func=AF.Silu),
                          reads=rz, writes=[zr])
                    ta, tr = t1.next()
                    halo = [("cu", nb - 1)] if nb > 0 else ["cu_pad"]
                    P.add("pool", lambda e, ta=ta, c0=c0, j=j: e.tensor_scalar(
                        out=ta[:], in0=cu[:, c0:c0 + 512], scalar1=cw[:, 0, j:j + 1], scalar2=0.0,
                        op0=ALU.mult, op1=ALU.add), reads=[("cu", nb), ("sc_cw", 0)] + halo, writes=[tr])
                    P.add("dve", lambda e, ta=ta, c0=c0, j=j: e.scalar_tensor_tensor(
                        out=ta[:], in0=cu[:, 1 + c0:1 + c0 + 512], scalar=cw[:, 1, j:j + 1], in1=ta[:],
                        op0=ALU.mult, op1=ALU.add), reads=[("cu", nb), ("sc_cw", 1), tr] + halo, writes=[tr])
                    P.add("dve", lambda e, ta=ta, c0=c0, j=j: e.scalar_tensor_tensor(
                        out=ta[:], in0=cu[:, 2 + c0:2 + c0 + 512], scalar=cw[:, 2, j:j + 1], in1=ta[:],
                        op0=ALU.mult, op1=ALU.add), reads=[("cu", nb), ("sc_cw", 2), tr], writes=[tr])
                    ga, gr = gg.next()
                    P.add("dve", lambda e, ga=ga, pB=pB, za=za: e.tensor_tensor(
                        out=ga[:], in0=pB, in1=za, op=ALU.mult), reads=rB + [zr], writes=[gr])
                    P.add("pool", lambda e, ga=ga, ta=ta, j=j, c0=c0: e.tensor_tensor(
                        out=yT[:, j, c0:c0 + 512], in0=ta[:], in1=ga[:], op=ALU.mult),
                        reads=[tr, gr], writes=[("yT", j, nb)])

        def phase_out(l, wout_d, final, xs, sb, scrA, scrB):
            P.add("sp", lambda e: e.dma_start(out=wn[:], in_=post_d[l].partition_broadcast(128)),
                  writes=["wn"], dma=True)
            wsrc = wout_d.rearrange("(j p) c -> p j c", p=128)
            for h in range(16):
                P.add("pool", lambda e, h=h: e.dma_start(out=wout[:, h:h + 1, :], in_=wsrc[:, h:h + 1, :]),
                      writes=(hT_all if h == 0 else []) + [("wout", h)], dma=True)
            ssq = sb(f"ssq{l}", [128, NT])
            sdq = sb(f"sdq{l}", [128, NT])
            rsq = sb(f"rsq{l}", [128, NT])
            tmps = [(scrA, [("scrA", 0), ("scrA", 1)]), (scrB, [("scrB", 0), ("scrB", 1)])]
            for t in range(NT):
                b0 = (t % 2) * 2
                for half in range(2):
                    for j in range(16):
                        P.add("pe", lambda e, t=t, half=half, j=j, b0=b0: e.matmul(
                            pbank(b0 + half), lhsT=yT[:, j, t * 128:(t + 1) * 128],
                            rhs=wout[:, j, half * 512:(half + 1) * 512], start=(j == 0), stop=(j == 15)),
                            reads=[("wout", j), ("wout", 0), ("yT", j, t // 4)] + (hT_all if j == 15 else []),
                            writes=PS(b0 + half))
                pp = psum[:, b0:b0 + 2, :].rearrange("p a b -> p (a b)")
                pr = PS(b0) + PS(b0 + 1)
                P.add("act", lambda e, t=t, pp=pp: e.activation(out=junk[:], in_=pp, func=AF.Square,
                                                                accum_out=ssq[:, t:t + 1]),
                      reads=pr, writes=["junk", ("ssq", t)])
                P.add("act", lambda e, t=t: e.activation(out=sdq[:, t:t + 1], in_=ssq[:, t:t + 1], func=AF.Sqrt,
                                                         scale=1.0 / D, bias=EPS),
                      reads=[("ssq", t)], writes=[("sdq", t)])
                P.add("dve", lambda e, t=t: e.reciprocal(out=rsq[:, t:t + 1], in_=sdq[:, t:t + 1]),
                      reads=[("sdq", t)], writes=[("rsq", t)])
                ta, tr = tmps[t % 2]
                P.add("dve", lambda e, t=t, ta=ta, pp=pp: e.scalar_tensor_tensor(
                    out=ta[:], in0=pp, scalar=rsq[:, t:t + 1], in1=wn[:], op0=ALU.mult, op1=ALU.mult),
                    reads=pr + [("rsq", t), "wn"], writes=tr)
                P.add("pool", lambda e, t=t, ta=ta: e.tensor_tensor(
                    out=xs[:, t, :], in0=xs[:, t, :], in1=ta[:], op=ALU.add),
                    reads=[("xs", t)] + tr, writes=[("xs", t)])
                if final:
                    P.add("sp", lambda e, t=t: e.dma_start(out=out_d[t * 128:(t + 1) * 128, :], in_=xs[:, t, :]),
                          reads=[("xs", t)], dma=True, is_out=True)

        def phase_gdn(sb, xbufs):
            BIG = 1.0e4
            gsrc = gwin_d.rearrange("(kc p) n -> p kc n", p=128)
            onesF = sb("onesF", [128, 128])
            Tri = sb("Tri", [128, 128])
            maskL = sb("maskL", [128, 128])
            maskU = sb("maskU", [128, 128])
            GR = sb("g_GR", [128, 4, 128])
            Eb = GR[:].rearrange("p a b -> p (a b)").bitcast(BF16).rearrange("p (a b) -> p a b", b=128)
            maskB = sb("maskB", [128, 7, 4, 128], BF16)
            UT0 = sb("UT0", [128, 4, 2, 128], BF16)
            UTw = [sb(f"UTw{i}", [128, 4, 2, 128], BF16) for i in range(2)]
            zerosF = sb("zerosF", [128, 128])
            sel = lambda out, in_, pat, op, fill, base, cm, reads, writes: P.add(
                "pool", lambda e: e.affine_select(out=out, in_=in_, pattern=pat, compare_op=op, fill=fill,
                                                  base=base, channel_multiplier=cm), reads, writes)
            P.add("pool", lambda e: e.memset(onesF[:], 1.0), writes=["onesF"])
            P.add("pool", lambda e: e.memset(zerosF[:], 0.0), writes=["zerosF"])
            sel(Tri[:], onesF[:], [[1, 128]], ALU.is_ge, 0.0, 0, -1, ["onesF"], ["Tri"])
            sel(maskL[:], zerosF[:], [[-1, 128]], ALU.is_ge, BIG, 0, 1, ["zerosF"], ["maskL"])
            sel(maskU[:], zerosF[:], [[1, 128]], ALU.is_ge, -BIG, 0, -1, ["zerosF"], ["maskU"])
            P.add("pool", lambda e: e.tensor_copy(out=Eb[:, 0, :], in_=ident[:]), reads=["ident"], writes=[("Eb", 0)])
            P.add("pool", lambda e: e.memset(Eb[:, 7, :], 1.0), writes=[("Eb", 7)])
            for k in range(1, 7):
                b = 1 << k
                sel(Eb[:, k, :], onesF[:], [[-b, 128 // b], [0, b]], ALU.is_ge, 0.0, 0, 1, ["onesF"], [("Eb", k)])
                sel(Eb[:, k, :], Eb[:, k, :], [[b, 128 // b], [0, b]], ALU.is_ge, 0.0, b - 1, -1,
                    [("Eb", k)], [("Eb", k)])
            for lv in range(7):
                for i in range(4):
                    TT("pool", maskB[:, lv, i, :], Eb[:, lv + 1, :], Eb[:, lv, :], ALU.subtract,
                       [("Eb", lv + 1), ("Eb", lv)], [("maskB", lv)] if i == 3 else [("maskB_part", lv, i)])
            for a in range(4):
                for b_ in range(2):
                    P.add("pool", lambda e, a=a, b_=b_: e.tensor_copy(out=UT0[:, a, b_, :], in_=ident[:]),
                          reads=["ident"], writes=["UT0"] if (a, b_) == (3, 1) else [("UT0_part", a, b_)])

            gnw = sb("gnw", [128, 1])
            P.add("sp", lambda e: e.dma_start(out=gnw[:], in_=gnw_d), writes=["gnw"], dma=True)
            gcw = sb("gcw", [128, 4, 32])
            for k in range(4):
                P.add("sp", lambda e, k=k: e.dma_start(out=gcw[:, k, :], in_=gcw_d[k].rearrange("(j p) -> p j", p=128),
                                                       allow_slow_non_contiguous=True),
                      writes=[("gcw", k)], dma=True)
            gcwr = [("gcw", k) for k in range(4)]

            sm = lambda n: sb(n, [128, 16, 16])
            beta, tmp1, dtb, eA, graw, cg, gl, eg, bgam, kd, sdec = [sm(n) for n in (
                "g_beta", "g_tmp1", "g_dtb", "g_eA", "g_graw", "g_cg", "g_gl", "g_eg", "g_bgam", "g_kd", "g_sdec")]
            fl = lambda a: a[:].rearrange("p a b -> p (a b)")
            P.add("sp", lambda e: e.dma_start(out=dtb[:], in_=bass.AP(gdtb_d.tensor, 0, [[0, 128], [0, 16], [1, 16]])),
                  writes=["g_dtb"], dma=True)
            P.add("sp", lambda e: e.dma_start(out=eA[:], in_=bass.AP(galog_d.tensor, 0, [[0, 128], [0, 16], [1, 16]])),
                  writes=["g_eA"], dma=True)
            wbg = sb("wbg", [128, 8, 32], BF16)
            P.add("pool", lambda e: e.dma_start(out=wbg[:], in_=gsrc[:, :, 6144:6176]), writes=["wbg"], dma=True)
            for c in range(16):
                for kc in range(8):
                    MM(psum[:, 0, c * 32:(c + 1) * 32], hT[:, kc, c * 128:(c + 1) * 128], wbg[:, kc, :],
                       [("hT", c), "wbg"], PS(0), start=(kc == 0), stop=(kc == 7))
            pbg = psum[:, 0, :].rearrange("p (c n) -> p c n", n=32)
            ACTF(beta[:], pbg[:, :, 0:16], AF.Sigmoid, PS(0), ["g_beta"])
            TT("dve", tmp1[:], pbg[:, :, 16:32], dtb[:], ALU.add, PS(0) + ["g_dtb"], ["g_tmp1"])
            ACTF(tmp1[:], tmp1[:], AF.Exp, ["g_tmp1"], ["g_tmp1"])
            ACTF(tmp1[:], tmp1[:], AF.Ln, ["g_tmp1"], ["g_tmp1"], bias=1.0)
            ACTF(eA[:], eA[:], AF.Exp, ["g_eA"], ["g_eA"])
            STT(graw[:], tmp1[:], -1.0, eA[:], ALU.mult, ALU.mult, ["g_tmp1", "g_eA"], ["g_graw"])
            MM(psum[:, 1, 0:256], Tri[:], fl(graw), ["Tri", "g_graw"], PS(1))
            MM(psum[:, 2, 0:256], onesF[:], fl(graw), ["onesF", "g_graw"], PS(2))
            ACTF(fl(cg), psum[:, 1, 0:256], AF.Copy, PS(1), ["g_cg"])
            ACTF(fl(gl), psum[:, 2, 0:256], AF.Copy, PS(2), ["g_gl"])
            ACTF(fl(eg), psum[:, 1, 0:256], AF.Exp, PS(1), ["g_eg"])
            ACTF(fl(sdec), psum[:, 2, 0:256], AF.Exp, PS(2), ["g_sdec"])
            TT("dve", fl(gl), fl(gl), fl(cg), ALU.subtract, ["g_gl", "g_cg"], ["g_gl"])
            ACTF(fl(kd), fl(gl), AF.Exp, ["g_gl"], ["g_kd"])
            TT("dve", fl(bgam), fl(beta), fl(eg), ALU.mult, ["g_beta", "g_eg"], ["g_bgam"])

            wrot = Rot("g_w", [sb(f"g_w{i}", [128, 8, 128], BF16) for i in range(3)])
            wz = [sb(f"g_wz{i}", [128, 8, 128], BF16) for i in range(2)]
            craw = sb("g_craw", [128, 3 + T])
            P.add("pool", lambda e: e.memset(craw[:, 0:3], 0.0), writes=["craw_pad"])
            xh = lambda k, i: xbufs[k][:, i * 512:(i + 1) * 512]
            t1 = Rot("xsA0", [xh(0, 0)])
            s1 = Rot("xsA0b", [xh(0, 1)])
            s1.next = lambda: (xh(0, 1), ("xsA0", 1))
            o_za = (xh(1, 0), ("xsA1", 0))
            o_oa = (xh(1, 1), ("xsA1", 1))
            o_qa = (wn[:, 0:512], "wn")
            o_ra = (wn[:, 512:1024], "wn")
            sq = Rot("g_sq", [sb(f"g_sq{i}", [128, 512]) for i in range(1)])
            rn = Rot("g_rn", [sb(f"g_rn{i}", [128, 512]) for i in range(1)])
            qT = sb("g_qT", [128, T], BF16)
            kT = sb("g_kT", [128, T], BF16)
            vT = [sb(f"g_vT{i}", [128, T], BF16) for i in range(2)]
            pj_bank = Rot("pjb", [0, 1])
            NI = 4
            b4 = lambda n: sb(n, [128, NI, 128], BF16)
            f4 = lambda n: sb(n, [128, NI, 128])
            ndL = f4("g_ndL")
            ndU = f4("g_ndU")
            grow = f4("g_grow")
            Abuf = b4("g_A")
            Msb = b4("g_Msb")
            kbg = b4("g_kbg")
            vb = b4("g_vb")
            qkmT = Rot("g_qkmT", [b4(f"g_qkmT{i}") for i in range(2)])
            qdT = Rot("g_qdT", [b4(f"g_qdT{i}") for i in range(2)])
            kdec = Rot("g_kdec", [b4(f"g_kdec{i}") for i in range(2)])
            u_sb = Rot("g_u", [f4(f"g_u{i}") for i in range(2)])
            wT_sb = Rot("g_wT", [b4(f"g_wT{i}") for i in range(2)])
            vnew = Rot("g_vnew", [sb(f"g_vnew{i}", [128, 2, 128], BF16) for i in range(2)])
            S = sb("g_S", [128, 2, 128])
            Sbf = sb("g_Sbf", [128, 2, 128], BF16)
            pT1 = pbank(1).bitcast(BF16)
            QSCALE = 128.0 ** -0.5
            print("gdn sbuf bytes remaining", nc.sbuf_bytes_remaining)

            def proj_tile(col0, kind, dest, dres):
                j = col0 // 128
                wa, wr = wrot.next()
                P.add("pool", lambda e: e.dma_start(out=wa[:], in_=gsrc[:, :, col0:col0 + 128]), writes=[wr], dma=True)
                for nb in range(4):
                    c0 = nb * 512
                    hres = [("hT", 4 * nb + i) for i in range(4)]
                    bk, _ = pj_bank.next()
                    for kc in range(8):
                        MM(pbank(bk), wa[:, kc, :], hT[:, kc, c0:c0 + 512], [wr] + hres, PS(bk),
                           start=(kc == 0), stop=(kc == 7))
                    ACTF(craw[:, 3 + c0:3 + c0 + 512], pbank(bk), AF.Copy, PS(bk), [("craw", nb)])
                    yield
                    halo = [("craw", nb - 1)] if nb > 0 else ["craw_pad"]
                    ta, tr = t1.next()
                    TS("pool", ta, craw[:, c0:c0 + 512], gcw[:, 0, j:j + 1], 0.0, ALU.mult, ALU.add,
                       [("craw", nb)] + halo + gcwr, [tr])
                    for k in (1, 2, 3):
                        STT(ta, craw[:, c0 + k:c0 + k + 512], gcw[:, k, j:j + 1], ta, ALU.mult, ALU.add,
                            [("craw", nb), tr] + halo + gcwr, [tr])
                    yield
                    if kind == "v":
                        ACTF(dest[:, c0:c0 + 512], ta, AF.Silu, [tr], [dres + (nb,)])
                        continue
                    sa, sr = s1.next()
                    ACTF(sa, ta, AF.Silu, [tr], [sr])
                    qa, qr = sq.next()
                    ACTF(qa[:], sa, AF.Square, [sr], [qr])
                    bk2, _ = pj_bank.next()
                    yield
                    MM(pbank(bk2), onesF[:], qa[:], ["onesF", qr], PS(bk2))
                    ra, rr = rn.next()
                    ACTF(ra[:], pbank(bk2), AF.Ln, PS(bk2), [rr], bias=EPS)
                    ACTF(ra[:], ra[:], AF.Exp, [rr], [rr], scale=-0.5)
                    yield
                    STT(dest[:, c0:c0 + 512], sa, (QSCALE if kind == "q" else 1.0), ra[:], ALU.mult, ALU.mult,
                        [sr, rr], [dres + (nb,)])

            def bufset(p):
                if p % 2 == 1 and p < 6:
                    aps = [yT[:, 12 + k, :] for k in range(4)]
                    res = [("yT", 12 + k) for k in range(4)]
                else:
                    aps = [qT[:], kT[:], vT[0][:], vT[1][:]]
                    res = [("g_qT",), ("g_kT",), ("g_vT0",), ("g_vT1",)]
                return aps, res

            def proj_gen(p):
                aps, res = bufset(p)
                yield from proj_tile(p * 128, "q", aps[0], res[0])
                yield from proj_tile(1024 + p * 128, "k", aps[1], res[1])
                for hh in range(2):
                    yield from proj_tile(2048 + (2 * p + hh) * 128, "v", aps[2 + hh], res[2 + hh])

            def drive(gens):
                gens = list(gens)
                while gens:
                    for g in list(gens):
                        try:
                            next(g)
                        except StopIteration:
                            gens.remove(g)

            v3 = lambda ap: ap.rearrange("p (a b) -> p a b", b=128)
            live01 = [0]

            def drive_bg(gens, bg, ratio=3):
                gens = list(gens)
                n = 0
                while gens:
                    for g in list(gens):
                        try:
                            next(g)
                        except StopIteration:
                            gens.remove(g)
                        n += 1
                        if bg[0] is not None and n % ratio == 0 and live01[0] == 0:
                            try:
                                next(bg[0])
                            except StopIteration:
                                bg[0] = None

            overlap_ok = lambda pn: pn < DBG_NPAIRS and bufset(pn)[1] != bufset(pn - 1)[1]
            drive([proj_gen(0)])
            for p in range(DBG_NPAIRS):
                hs = (2 * p, 2 * p + 1)
                (qA, kA, v0A, v1A), (qN, kN, v0N, v1N) = bufset(p)
                vA, vN = (v0A, v1A), (v0N, v1N)
                for hh in range(2):
                    zsrc = gsrc[:, :, 4096 + hs[hh] * 128:4096 + (hs[hh] + 1) * 128]
                    P.add("pool", lambda e, hh=hh, zsrc=zsrc: e.dma_start(out=wz[hh][:], in_=zsrc),
                          writes=[("g_wz", hh)], dma=True)
                bg = [proj_gen(p + 1) if overlap_ok(p + 1) else None]
                P.add("pool", lambda e: e.memset(S[:], 0.0), writes=[("g_S", 0), ("g_S", 1)])
                P.add("pool", lambda e: e.memset(Sbf[:], 0.0), writes=["g_Sbf"])
                st_ = {}
                if DBG_STAGE < 2:
                    continue

                def prep2(cp):
                    insts = [(cb, hh) for cb in range(2) for hh in range(2)]
                    live01[0] += 1
                    for cb in range(2):
                        c = 2 * cp + cb
                        cs = slice(c * 128, (c + 1) * 128)
                        nb = c // 4
                        qres, kres = qN + (nb,), kN + (nb,)
                        MM(psum[:, 0, cb * 256:cb * 256 + 128], kA[:, cs], kA[:, cs], [kres], PS(0))
                        MM(psum[:, 0, cb * 256 + 128:cb * 256 + 256], kA[:, cs], qA[:, cs], [kres, qres], PS(0))
                        TR(pT1[:, cb * 128:(cb + 1) * 128], kA[:, cs], [kres], PS(1))
                        for hh in range(2):
                            i = cb * 2 + hh
                            TR(pT1[:, 256 + i * 128:256 + (i + 1) * 128], vA[hh][:, cs], [vN[hh] + (nb,)], PS(1))
                    for i, (cb, hh) in enumerate(insts):
                        c = 2 * cp + cb
                        TS("pool", GR[:, i, :], onesF[:], graw[:, c, hs[hh]:hs[hh] + 1], 0.0, ALU.mult, ALU.add,
                           ["onesF", "g_graw"], [("g_GR", i)])
                        MM(psum[:, 2, i * 128:(i + 1) * 128], GR[:, i, :], Tri[:], [("g_GR", i), "Tri"], PS(2))
                    yield
                    for i, (cb, hh) in enumerate(insts):
                        c, h = 2 * cp + cb, hs[hh]
                        gb = psum[:, 2, i * 128:(i + 1) * 128]
                        STT(ndL[:, i, :], gb, cg[:, c, h:h + 1], maskL[:], ALU.subtract, ALU.max,
                            PS(2) + ["g_cg", "maskL"], [("g_ndL", i)])
                        STT(ndU[:, i, :], gb, cg[:, c, h:h + 1], maskU[:], ALU.subtract, ALU.min,
                            PS(2) + ["g_cg", "maskU"], [("g_ndU", i)])
                    allr = lambda n: [(n, i) for i in range(NI)]
                    ACTF(grow[:], v3(pbank(2)), AF.Exp, PS(2), allr("g_grow"))
                    ACTF(ndL[:], ndL[:], AF.Exp, allr("g_ndL"), allr("g_ndL"), scale=-1.0)
                    ACTF(ndU[:], ndU[:], AF.Exp, allr("g_ndU"), allr("g_ndU"))
                    yield
                    qk_a, qk_r = qkmT.next()
                    qd_a, qd_r = qdT.next()
                    kd_a, kd_r = kdec.next()
                    for i, (cb, hh) in enumerate(insts):
                        c, h = 2 * cp + cb, hs[hh]
                        cs = slice(c * 128, (c + 1) * 128)
                        STT(Abuf[:, i, :], psum[:, 0, cb * 256:cb * 256 + 128], beta[:, c, h:h + 1], ndL[:, i, :],
                            ALU.mult, ALU.mult, PS(0) + ["g_beta", ("g_ndL", i)], [("g_A", i)])
                        TT("dve", qk_a[:, i, :], psum[:, 0, cb * 256 + 128:cb * 256 + 256], ndU[:, i, :], ALU.mult,
                           PS(0) + [("g_ndU", i)], [(qk_r, i)])
                        TT("pool", qd_a[:, i, :], qA[:, cs], grow[:, i, :], ALU.mult,
                           [qN + (c // 4,), ("g_grow", i)], [(qd_r, i)])
                        AMUL(kbg[:, i, :], pT1[:, cb * 128:(cb + 1) * 128], bgam[:, c, h:h + 1],
                             PS(1) + ["g_bgam"], [("g_kbg", i)])
                        AMUL(kd_a[:, i, :], pT1[:, cb * 128:(cb + 1) * 128], kd[:, c, h:h + 1],
                             PS(1) + ["g_kd"], [(kd_r, i)])
                        AMUL(vb[:, i, :], pT1[:, 256 + i * 128:256 + (i + 1) * 128], beta[:, c, h:h + 1],
                             PS(1) + ["g_beta"], [("g_vb", i)])
                    live01[0] -= 1
                    yield
                    cur, curr = UT0, [["UT0"], ["UT0"]]
                    p4 = [psum[:, 4, :].rearrange("p (a b c) -> p a b c", a=2, b=2),
                          psum[:, 7, :].rearrange("p (a b c) -> p a b c", a=2, b=2)]
                    mbank = (3, 2)
                    pbk = (4, 7)
                    for lv in range(7):
                        for half in range(2):
                            for k in range(2):
                                i = 2 * half + k
                                MM(psum[:, mbank[half], k * 128:(k + 1) * 128], Abuf[:, i, :], cur[:, i, 0, :],
                                   [("g_A", i)] + curr[half], PS(mbank[half]))
                        yield
                        for half in range(2):
                            TT("dve", Msb[:, 2 * half:2 * half + 2, :], v3(psum[:, mbank[half], 0:256]),
                               maskB[:, lv, 2 * half:2 * half + 2, :], ALU.mult,
                               PS(mbank[half]) + [("maskB", lv)], [("g_Msb", half)])
                            for k in range(2):
                                i = 2 * half + k
                                pp = p4[half]
                                MM(pp[:, k, 0, :], cur[:, i, 1, :], Msb[:, i, :], curr[half] + [("g_Msb", half)],
                                   PS(pbk[half]))
                                if lv < 6:
                                    MM(pp[:, k, 1, :], Msb[:, i, :], cur[:, i, 1, :], curr[half] + [("g_Msb", half)],
                                       PS(pbk[half]))
                        yield
                        nxt = UTw[lv % 2]
                        nxtr = [[("UTw", lv % 2, 0)], [("UTw", lv % 2, 1)]]
                        for half in range(2):
                            STT(nxt[:, 2 * half:2 * half + 2].rearrange("p a b c -> p (a b c)"),
                                pbank(pbk[half]), -1.0,
                                cur[:, 2 * half:2 * half + 2].rearrange("p a b c -> p (a b c)"),
                                ALU.mult, ALU.add, curr[half] + PS(pbk[half]), nxtr[half])
                        cur, curr = nxt, nxtr
                    curr = curr[0] + curr[1]
                    for i in range(NI):
                        MM(psum[:, 2, i * 128:(i + 1) * 128], cur[:, i, 0, :], vb[:, i, :], curr + [("g_vb", i)], PS(2))
                    for i in range(NI):
                        MM(psum[:, 3, i * 128:(i + 1) * 128], kbg[:, i, :], cur[:, i, 0, :], curr + [("g_kbg", i)], PS(3))
                    yield
                    u_a, u_r = u_sb.next()
                    w_a, w_r = wT_sb.next()
                    ACTF(u_a[:], v3(pbank(2)), AF.Copy, PS(2), [u_r])
                    ACTF(w_a[:], v3(pbank(3)), AF.Copy, PS(3), [w_r])
                    st_[cp] = dict(u=(u_a, u_r), w=(w_a, w_r), qk=(qk_a, qk_r), qd=(qd_a, qd_r), kd=(kd_a, kd_r))

                def recur(c):
                    d = st_[c // 2]
                    (u_a, u_r), (w_a, w_r), (qk_a, qk_r), (qd_a, qd_r), (kd_a, kd_r) = d["u"], d["w"], d["qk"], d["qd"], d["kd"]
                    cb = c % 2
                    i0_ = cb * 2
                    for hh in range(2):
                        MM(psum[:, 5, hh * 128:(hh + 1) * 128], w_a[:, i0_ + hh, :], Sbf[:, hh, :], [w_r, "g_Sbf"], PS(5))
                    yield
                    vn_a, vn_r = vnew.next()
                    STT(vn_a[:].rearrange("p a b -> p (a b)"), psum[:, 5, 0:256], -1.0,
                        u_a[:, i0_:i0_ + 2, :].rearrange("p a b -> p (a b)"), ALU.mult, ALU.add, [u_r] + PS(5), [vn_r])
                    for hh in range(2):
                        ocol = hh * 256 + cb * 128
                        MM(psum[:, 6, ocol:ocol + 128], Sbf[:, hh, :], qd_a[:, i0_ + hh, :], ["g_Sbf", (qd_r, i0_ + hh)],
                           PS(6), start=True, stop=False)
                        MM(psum[:, 6, ocol:ocol + 128], vn_a[:, hh, :], qk_a[:, i0_ + hh, :], [vn_r, (qk_r, i0_ + hh)],
                           PS(6), start=False, stop=True)
                        MM(psum[:, 5, 256 + hh * 128:256 + (hh + 1) * 128], kd_a[:, i0_ + hh, :], vn_a[:, hh, :],
                           [(kd_r, i0_ + hh), vn_r], PS(5))
                    yield
                    for hh in range(2):
                        STT(S[:, hh, :], S[:, hh, :], sdec[:, c, hs[hh]:hs[hh] + 1],
                            psum[:, 5, 256 + hh * 128:256 + (hh + 1) * 128], ALU.mult, ALU.add,
                            [("g_S", hh), "g_sdec"] + PS(5), [("g_S", hh)])
                    ACTF(Sbf[:], S[:], AF.Copy, [("g_S", 0), ("g_S", 1)], ["g_Sbf"])
                    if cb == 1:
                        t0_ = (c - 1) * 128
                        hres = [("hT", c - 1), ("hT", c)]
                        yield
                        live01[0] += 1
                        for hh in range(2):
                            for kc in range(8):
                                MM(psum[:, 0, hh * 256:(hh + 1) * 256], wz[hh][:, kc, :], hT[:, kc, t0_:t0_ + 256],
                                   [("g_wz", hh)] + hres, PS(0), start=(kc == 0), stop=(kc == 7))
                        qa, qr = o_qa
                        ACTF(qa, pbank(6), AF.Square, PS(6), [qr])
                        yield
                        MM(pbank(1), onesF[:], qa, ["onesF", qr], PS(1))
                        za, zr = o_za
                        ACTF(za, pbank(0), AF.Silu, PS(0), [zr])
                        yield
                        ra, rr = o_ra
                        ACTF(ra, pbank(1), AF.Ln, PS(1), [rr], scale=1.0 / 128, bias=EPS)
                        live01[0] -= 1
                        yield
                        ACTF(ra, ra, AF.Exp, [rr], [rr], scale=-0.5)
                        yield
                        oa, orr = o_oa
                        STT(oa, pbank(6), gnw[:, 0:1], ra, ALU.mult, ALU.mult, PS(6) + ["gnw", rr], [orr])
                        yield
                        TT("pool", yT[:, hs[0]:hs[0] + 2, t0_:t0_ + 256], oa.rearrange("p (a b) -> p a b", a=2),
                           za.rearrange("p (a b) -> p a b", a=2), ALU.mult, [orr, zr],
                           [("yT", hs[0], c // 4), ("yT", hs[1], c // 4)])

                def tail(cp):
                    yield from recur(2 * cp)
                    yield
                    yield from recur(2 * cp + 1)

                ncp = DBG_NCHUNK // 2
                drive_bg([prep2(0)], bg)
                for cp in range(ncp):
                    gens = [tail(cp)]
                    if cp + 1 < ncp:
                        gens.insert(0, prep2(cp + 1))
                    drive_bg(gens, bg)
                if bg[0] is not None:
                    drive([bg[0]])
                elif p + 1 < DBG_NPAIRS and not overlap_ok(p + 1):
                    drive([proj_gen(p + 1)])

        def load_xs(xs):
            for t in range(NT):
                P.add("sp", lambda e, t=t: e.dma_start(out=xs[:, t, :], in_=x_d[t * 128:(t + 1) * 128, :]),
                      writes=[("xs", t)], dma=True)

        xs = None
        scr = None
        for li, l in enumerate(layers):
            final = li == len(layers) - 1
            if l == 0:
                with ExitStack() as s0:
                    sb0 = mk_sb(s0)
                    xbufs = [sb0(f"xstream{i}", [128, D]) for i in range(2)]

                    def xsrc(t):
                        k = t % 2
                        P.add("sp", lambda e, t=t, k=k: e.dma_start(out=xbufs[k][:], in_=x_d[t * 128:(t + 1) * 128, :]),
                              writes=[(f"xsA{k}", 0), (f"xsA{k}", 1)], dma=True)
                        return xbufs[k][:], [(f"xsA{k}", 0), (f"xsA{k}", 1)]
                    phase_norm(0, xsrc)
                    phase_gdn(sb0, xbufs)
                P.barrier()
                xs = sb("xs", [128, NT, D])
                scr = (sb("scrA", [128, D]), sb("scrB", [128, D]))
                load_xs(xs)
                phase_out(0, gwout_d, final, xs, sb, *scr)
            else:
                if xs is None:
                    xs = sb("xs", [128, NT, D])
                    scr = (sb("scrA", [128, D]), sb("scrB", [128, D]))
                    load_xs(xs)
                phase_norm(1, lambda t: (xs[:, t, :], [("xs", t)]), stats_first=True)
                phase_sc(sb, *scr)
                phase_out(1, swout_d, final, xs, sb, *scr)
        P.emit(st)
        global LAST_P
        LAST_P = P
    return nc


_NC_CACHE = {}


def _get_nc(layers):
    if layers not in _NC_CACHE:
        _NC_CACHE[layers] = build_program(layers)
    return _NC_CACHE[layers]


def _in_maps(inputs, xin):
    f = lambda a: np.ascontiguousarray(np.asarray(a, dtype=np.float32))
    shared = {
        "pre_norm_w": f(inputs["pre_norm_w"]),
        "post_norm_w": f(inputs["post_norm_w"]),
        "gdn_w_in": f(inputs["gdn_w_in"][0]),
        "gdn_conv_w": f(inputs["gdn_conv_w"][0]),
        "gdn_A_log": f(inputs["gdn_A_log"]).reshape(1, 16),
        "gdn_dt_bias": f(inputs["gdn_dt_bias"]).reshape(1, 16),
        "gdn_norm_w": f(inputs["gdn_norm_w"]).reshape(128, 1),
        "gdn_w_out": f(inputs["gdn_w_out"][0]),
        "sc_w_in": f(inputs["sc_w_in"][0]),
        "sc_conv_w": f(inputs["sc_conv_w"][0]),
        "sc_w_out": f(inputs["sc_w_out"][0]),
    }
    return [dict(shared, x=f(xin[b])) for b in range(8)]


def run_layers(inputs, layers, xin=None):
    x = np.asarray(inputs["x"], dtype=np.float32) if xin is None else xin
    nc = _get_nc(tuple(layers))
    res = run_bass_kernel_spmd(nc, _in_maps(inputs, x), core_ids=list(range(8)))
    return np.stack([np.asarray(r["out"], dtype=np.float32) for r in res.results], axis=0)


FUSED = True


def kernel(**inputs):
    if FUSED:
        return run_layers(inputs, (0, 1))
    x1 = run_layers(inputs, (0,))
    return run_layers(inputs, (1,), xin=x1)
```
